# Optimizing a Trainium2 kernel written in Bass

```python
import math
import jax
import jax.numpy as jnp
from jax import lax
import numpy as np

D_MODEL = 2048
BATCH = 8
SEQ = 2048
DEPTH = 1

MIX_WIDTH = D_MODEL
S5_WIDTH = MIX_WIDTH // 2
S5_GROUP = 16
S5_GROUPS = S5_WIDTH // S5_GROUP
S5_STATE = 64
RWKV_WIDTH = MIX_WIDTH - S5_WIDTH
RWKV_HEAD = 64
RWKV_HEADS = RWKV_WIDTH // RWKV_HEAD
DECAY_LORA = max(32, int(round(1.8 * math.sqrt(RWKV_WIDTH) / 32)) * 32)
ICLR_LORA = DECAY_LORA
GATE_LORA = max(32, int(round(0.6 * RWKV_WIDTH ** 0.8 / 32)) * 32)
N_DIR = 2
RWKV_SPLITS = (RWKV_WIDTH, RWKV_WIDTH, RWKV_WIDTH, N_DIR * DECAY_LORA, N_DIR * ICLR_LORA, GATE_LORA)
RWKV_IN = sum(RWKV_SPLITS)
PROJ_WIDTH = S5_WIDTH + RWKV_IN
FFN_HIDDEN = 4 * D_MODEL
N_MOD = 6
NORM_EPS = 1e-6
GN_EPS = 64e-5
L2_EPS = 1e-12

kernel_name = 'hymba_s5_rwkv7_adaln_encoder_block'


def rms_norm(x, gain):
    xf = x.astype(jnp.float32)
    y = xf * lax.rsqrt(jnp.mean(xf * xf, axis=-1, keepdims=True) + NORM_EPS)
    return (y * gain).astype(x.dtype)


def modulate(h, shift, scale):
    return h * (1.0 + scale[:, None, :]) + shift[:, None, :]


def centred_token_shift(p, mu_prev, mu_next):
    prev = jnp.pad(p[:, :-1], ((0, 0), (1, 0), (0, 0)))
    nxt = jnp.pad(p[:, 1:], ((0, 0), (0, 1), (0, 0)))
    return p + mu_prev * (prev - p) + mu_next * (nxt - p)


def _complex_scan_combine(earlier, later):
    a1r, a1i, b1r, b1i = earlier
    a2r, a2i, b2r, b2i = later
    return (a1r * a2r - a1i * a2i,
            a1r * a2i + a1i * a2r,
            a2r * b1r - a2i * b1i + b2r,
            a2r * b1i + a2i * b1r + b2i)


def s5_mixer(u, lambda_re, lambda_im, log_step, b_re, b_im, c_re, c_im, d_skip, w_glu, b_glu):
    bsz, seq, _ = u.shape
    ug = u.reshape(bsz, seq, S5_GROUPS, S5_GROUP)
    states_re = []
    states_im = []
    for d in range(N_DIR):
        step = jnp.exp(log_step[d])[:, None]
        lr, li = lambda_re[d], lambda_im[d]
        mag = jnp.exp(lr * step)
        lbar_re = mag * jnp.cos(li * step)
        lbar_im = mag * jnp.sin(li * step)
        den = lr * lr + li * li
        nr = lbar_re - 1.0
        ni = lbar_im
        coef_re = ((nr * lr + ni * li) / den)[..., None]
        coef_im = ((ni * lr - nr * li) / den)[..., None]
        bbar_re = coef_re * b_re - coef_im * b_im
        bbar_im = coef_re * b_im + coef_im * b_re
        bu_re = jnp.einsum('bsgh,gph->bsgp', ug, bbar_re)
        bu_im = jnp.einsum('bsgh,gph->bsgp', ug, bbar_im)
        a_re = jnp.broadcast_to(lbar_re, (1, seq, S5_GROUPS, S5_STATE))
        a_im = jnp.broadcast_to(lbar_im, (1, seq, S5_GROUPS, S5_STATE))
        _, _, s_re, s_im = lax.associative_scan(
            _complex_scan_combine, (a_re, a_im, bu_re, bu_im), reverse=(d == 1), axis=1)
        states_re.append(s_re)
        states_im.append(s_im)
    x_re = states_re[0] + states_re[1]
    x_im = states_im[0] + states_im[1]
    y = jnp.einsum('bsgp,ghp->bsgh', x_re, c_re) - jnp.einsum('bsgp,ghp->bsgh', x_im, c_im)
    y = y.reshape(bsz, seq, S5_WIDTH) + d_skip * u
    y = jax.nn.gelu(y)
    return y * jax.nn.sigmoid(y @ w_glu + b_glu)


def _heads(t):
    return t.reshape(*t.shape[:-1], RWKV_HEADS, RWKV_HEAD)


def _shared_dir_time_major(t):
    t = jnp.stack([t, jnp.flip(t, axis=1)], axis=2)
    return jnp.transpose(t, (1, 2, 0, 3, 4))


def _dir_time_major(t):
    t = jnp.stack([t[:, :, 0], jnp.flip(t[:, :, 1], axis=1)], axis=2)
    return jnp.transpose(t, (1, 2, 0, 3, 4))


def _rwkv7_step(state, inp):
    r_t, w_t, k_t, v_t, kk_t, a_t = inp
    sa = jnp.einsum('dbhij,dbhj->dbhi', state, -kk_t)
    state = (state * w_t[..., None, :]
             + sa[..., :, None] * (kk_t * a_t)[..., None, :]
             + v_t[..., :, None] * k_t[..., None, :])
    y = jnp.einsum('dbhij,dbhj->dbhi', state, r_t)
    return state, y


def rwkv7_mixer(p, mu_prev, mu_next, w0, w_up, a0, a_up, g_up, k_k, k_a, r_k, ln_gain, ln_bias):
    bsz, seq, _ = p.shape
    p = centred_token_shift(p, mu_prev, mu_next)
    cut = [sum(RWKV_SPLITS[:i + 1]) for i in range(len(RWKV_SPLITS) - 1)]
    r, k, v, w_dn, a_dn, g_dn = jnp.split(p, cut, axis=-1)
    w_dn = w_dn.reshape(bsz, seq, N_DIR, DECAY_LORA)
    a_dn = a_dn.reshape(bsz, seq, N_DIR, ICLR_LORA)
    w = -jax.nn.softplus(-(w0 + jnp.einsum('bsdl,dlc->bsdc', jnp.tanh(w_dn), w_up))) - 0.5
    decay = jnp.exp(-jnp.exp(w))
    a = jax.nn.sigmoid(a0 + jnp.einsum('bsdl,dlc->bsdc', a_dn, a_up))
    g = jax.nn.sigmoid(g_dn) @ g_up
    kk = _heads(k * k_k).astype(jnp.float32)
    kk = kk / jnp.maximum(jnp.sqrt(jnp.sum(kk * kk, axis=-1, keepdims=True)), L2_EPS)
    k_dir = k[:, :, None, :] * (1.0 + (a - 1.0) * k_a)
    r_h, v_h = _heads(r), _heads(v)
    k_dir_h, decay_h, a_h = _heads(k_dir), _heads(decay), _heads(a)
    xs = (_shared_dir_time_major(r_h), _dir_time_major(decay_h), _dir_time_major(k_dir_h),
          _shared_dir_time_major(v_h), _shared_dir_time_major(kk), _dir_time_major(a_h))
    state0 = jnp.zeros((N_DIR, bsz, RWKV_HEADS, RWKV_HEAD, RWKV_HEAD), jnp.float32)
    _, ys = lax.scan(_rwkv7_step, state0, xs)
    ys = jnp.transpose(ys, (2, 0, 1, 3, 4))
    y = ys[:, :, 0] + jnp.flip(ys[:, :, 1], axis=1)
    mu = jnp.mean(y, axis=-1, keepdims=True)
    var = jnp.mean(jnp.square(y - mu), axis=-1, keepdims=True)
    y = ((y - mu) * lax.rsqrt(var + GN_EPS)).reshape(bsz, seq, RWKV_WIDTH) * ln_gain + ln_bias
    bonus_coef = jnp.sum(r_h[:, :, None] * k_dir_h * r_k, axis=(2, 4))[..., None]
    y = y + (bonus_coef * v_h).reshape(bsz, seq, RWKV_WIDTH)
    return y * g


def setup_inputs(seed: int = 0) -> dict:
    key = jax.random.key(seed)
    ks = iter(jax.random.split(key, 48))

    def nrm(shape, scale):
        return scale * jax.random.normal(next(ks), shape, jnp.float32)

    def unif(shape, lo, hi):
        return jax.random.uniform(next(ks), shape, jnp.float32, lo, hi)

    L, C, G, P, Hg = DEPTH, RWKV_WIDTH, S5_GROUPS, S5_STATE, S5_GROUP
    w0_base = -6.0 + 5.0 * jnp.linspace(0.0, 1.0, C) ** 0.85
    lam_im_base = jnp.pi * jnp.arange(P, dtype=jnp.float32)
    return {
        'x': nrm((BATCH, SEQ, D_MODEL), 1.0),
        'c': nrm((BATCH, D_MODEL), 1.0),
        'ada_w': nrm((L, D_MODEL, N_MOD * D_MODEL), 0.5 * D_MODEL ** -0.5),
        'ada_b': nrm((L, N_MOD * D_MODEL), 0.02),
        'norm1_gain': 1.0 + nrm((L, D_MODEL), 0.05),
        'norm2_gain': 1.0 + nrm((L, D_MODEL), 0.05),
        'final_gain': 1.0 + nrm((D_MODEL,), 0.05),
        'w_in': nrm((L, D_MODEL, PROJ_WIDTH), D_MODEL ** -0.5),
        'w_out': nrm((L, MIX_WIDTH, D_MODEL), MIX_WIDTH ** -0.5),
        's5_lambda_re': -0.5 + nrm((L, N_DIR, G, P), 0.01),
        's5_lambda_im': lam_im_base + nrm((L, N_DIR, G, P), 0.01),
        's5_log_step': unif((L, N_DIR, G), math.log(1e-3), math.log(1e-1)),
        's5_b_re': nrm((L, G, P, Hg), (2 * Hg) ** -0.5),
        's5_b_im': nrm((L, G, P, Hg), (2 * Hg) ** -0.5),
        's5_c_re': nrm((L, G, Hg, P), 0.5),
        's5_c_im': nrm((L, G, Hg, P), 0.5),
        's5_d': nrm((L, S5_WIDTH), 0.5),
        's5_w_glu': nrm((L, S5_WIDTH, S5_WIDTH), S5_WIDTH ** -0.5),
        's5_b_glu': nrm((L, S5_WIDTH), 0.02),
        'rk_shift_prev': unif((L, RWKV_IN), 0.1, 0.5),
        'rk_shift_next': unif((L, RWKV_IN), 0.1, 0.5),
        'rk_w0': w0_base + nrm((L, N_DIR, C), 0.1),
        'rk_w_up': nrm((L, N_DIR, DECAY_LORA, C), 0.1),
        'rk_a0': nrm((L, N_DIR, C), 0.1),
        'rk_a_up': nrm((L, N_DIR, ICLR_LORA, C), 0.5 * ICLR_LORA ** -0.5),
        'rk_g_up': nrm((L, GATE_LORA, C), GATE_LORA ** -0.5),
        'rk_k_k': 0.85 + nrm((L, C), 0.05),
        'rk_k_a': 1.0 + nrm((L, C), 0.05),
        'rk_r_k': nrm((L, RWKV_HEADS, RWKV_HEAD), 0.1),
        'rk_ln_gain': 1.0 + nrm((L, C), 0.05),
        'rk_ln_bias': nrm((L, C), 0.02),
        'ffn_w1': nrm((L, D_MODEL, FFN_HIDDEN), D_MODEL ** -0.5),
        'ffn_w2': nrm((L, FFN_HIDDEN, D_MODEL), FFN_HIDDEN ** -0.5),
    }


def reference(x, c, ada_w, ada_b, norm1_gain, norm2_gain, final_gain, w_in, w_out,
              s5_lambda_re, s5_lambda_im, s5_log_step, s5_b_re, s5_b_im, s5_c_re, s5_c_im,
              s5_d, s5_w_glu, s5_b_glu, rk_shift_prev, rk_shift_next, rk_w0, rk_w_up,
              rk_a0, rk_a_up, rk_g_up, rk_k_k, rk_k_a, rk_r_k, rk_ln_gain, rk_ln_bias,
              ffn_w1, ffn_w2):
    c_act = jax.nn.silu(c)
    for l in range(DEPTH):
        mod = c_act @ ada_w[l] + ada_b[l]
        shift1, scale1, gate1, shift2, scale2, gate2 = jnp.split(mod, N_MOD, axis=-1)
        h = modulate(rms_norm(x, norm1_gain[l]), shift1, scale1)
        proj = h @ w_in[l]
        u_s5, p_rwkv = jnp.split(proj, [S5_WIDTH], axis=-1)
        y_s5 = s5_mixer(u_s5, s5_lambda_re[l], s5_lambda_im[l], s5_log_step[l],
                        s5_b_re[l], s5_b_im[l], s5_c_re[l], s5_c_im[l], s5_d[l],
                        s5_w_glu[l], s5_b_glu[l])
        y_rk = rwkv7_mixer(p_rwkv, rk_shift_prev[l], rk_shift_next[l], rk_w0[l], rk_w_up[l],
                           rk_a0[l], rk_a_up[l], rk_g_up[l], rk_k_k[l], rk_k_a[l], rk_r_k[l],
                           rk_ln_gain[l], rk_ln_bias[l])
        mixed = jnp.concatenate([y_s5, y_rk], axis=-1) @ w_out[l]
        x = x + gate1[:, None, :] * mixed
        h = modulate(rms_norm(x, norm2_gain[l]), shift2, scale2)
        ffn = jnp.square(jax.nn.relu(h @ ffn_w1[l])) @ ffn_w2[l]
        x = x + gate2[:, None, :] * ffn
    return rms_norm(x, final_gain)
```

```python
import contextlib
import numpy as np
import ml_dtypes
import concourse.bass as bass
import concourse.mybir as mybir
from concourse.bass_utils import run_bass_kernel_spmd

F32 = mybir.dt.float32
BF16 = mybir.dt.bfloat16
ALU = mybir.AluOpType
AF = mybir.ActivationFunctionType
AX = mybir.AxisListType

D = 2048
S = 2048
NT = 16
KT = 16
PROJ = 4512
FFN = 8192
EPS = 1e-6

SAME_SYNC = ("act", "dve", "pool")


class Prog:
    ENG = ("pe", "act", "dve", "pool", "sp")

    def __init__(self, nc, stack):
        self.nc = nc
        self.stack = stack
        self.q = {e: [] for e in self.ENG}
        self.cnt = {e: 0 for e in self.ENG}
        self.sem = {e: stack.enter_context(nc.semaphore("s_" + e)) for e in self.ENG}
        self.seen = {e: {} for e in self.ENG}
        self.lastw = {}
        self.readers = {}
        self.dsem = {}
        self.dcnt = {}
        self.nsb = 0

    def sb(self, shape, dtype, name=None):
        self.nsb += 1
        return self.stack.enter_context(
            self.nc.sbuf_tensor("sb_" + (name or ("t%d" % self.nsb)), list(shape), dtype))

    GROUP = ("const", "dbg", "const2")

    def _tokval(self, tok):
        if tok[0] == "e":
            return tok[1], self.sem[tok[1]], tok[2]
        if tok[1] in self.GROUP:
            return "d:" + tok[1], self.dsem[tok[1]], 1 << 40
        return "d:" + tok[1], self.dsem[tok[1]], tok[2]

    def _deps(self, eng, reads, writes):
        need = {}

        def add(tok, war=False):
            name, sem, val = self._tokval(tok)
            if name == eng:
                if eng not in SAME_SYNC:
                    return
            if need.get(name, (None, 0))[1] < val:
                need[name] = (sem, val)

        for k in reads:
            t = self.lastw.get(k)
            if t:
                add(t)
        for k in writes:
            t = self.lastw.get(k)
            if t:
                add(t)
            for t in self.readers.get(k, {}).values():
                add(t, war=True)
        waits = []
        for name, (sem, val) in need.items():
            if self.seen[eng].get(name, 0) < val:
                self.seen[eng][name] = val
                waits.append((sem, val))
        return waits

    def _record(self, tok, reads, writes):
        name = self._tokval(tok)[0]
        for k in writes:
            self.lastw[k] = tok
            self.readers[k] = {}
        for k in reads:
            self.readers.setdefault(k, {})[name] = tok

    def op(self, eng, fn, reads=(), writes=()):
        waits = self._deps(eng, reads, writes)
        self.cnt[eng] += 1
        tok = ("e", eng, self.cnt[eng])
        self.q[eng].append((waits, fn, self.sem[eng], 1))
        self._record(tok, reads, writes)
        return tok

    def dma(self, eng, out, in_, slot, reads=(), writes=(), **kw):
        if slot not in self.dsem:
            self.dsem[slot] = self.stack.enter_context(self.nc.semaphore("d_" + slot))
            self.dcnt[slot] = 0
        waits = self._deps(eng, reads, writes)
        self.dcnt[slot] += 16
        tok = ("d", slot, self.dcnt[slot])
        self.q[eng].append((waits, lambda e: e.dma_start(out=out, in_=in_, **kw),
                            self.dsem[slot], 16))
        self._record(tok, reads, writes)
        return tok

    def barrier(self):
        for e in self.ENG:
            waits = []
            for e2 in self.ENG:
                if e2 != e and self.cnt[e2] > self.seen[e].get(e2, 0):
                    self.seen[e][e2] = self.cnt[e2]
                    waits.append((self.sem[e2], self.cnt[e2]))
            for slot, c in self.dcnt.items():
                name = "d:" + slot
                if c > self.seen[e].get(name, 0):
                    self.seen[e][name] = c
                    waits.append((self.dsem[slot], c))
            if waits:
                self.q[e].append((waits, None, None, 0))
        self.lastw = {}
        self.readers = {}

    def emit(self):
        nc = self.nc
        with nc.Block() as block:
            for name, deco in (("pe", block.tensor), ("act", block.scalar),
                               ("dve", block.vector), ("pool", block.gpsimd),
                               ("sp", block.sync)):
                def body(e, name=name):
                    for waits, fn, sem, inc in self.q[name]:
                        for s_, v_ in waits:
                            if v_ >= (1 << 40):
                                v_ = [self.dcnt[k] for k in self.GROUP if self.dsem.get(k) is s_][0]
                            e.wait_ge(s_, v_)
                        if fn is not None:
                            fn(e).then_inc(sem, inc)
                deco(body)


def build_program(debug=None):
    nc = bass.Bass("TRN2", target_bir_lowering=False)
    dbg = {}
    with contextlib.ExitStack() as stack:
        p = Prog(nc, stack)

        def din(name, shape, dt=F32):
            return nc.dram_tensor(name, list(shape), dt, kind="ExternalInput").ap()

        x_d = din("x", [S, D])
        c_d = din("c_col", [128, KT])
        adaw_d = din("ada_w", [128, 12, KT, 1024])
        adab_fm_d = din("ada_b_fm", [128, 96])
        adab_bc_d = din("ada_b_bc", [128, 2 * D])
        g1_d = din("g1_fm", [128, KT])
        g2_d = din("g2_fm", [128, KT])
        ident_d = din("ident", [128, 128])
        out_d = nc.dram_tensor("out", [S, D], F32, kind="ExternalOutput").ap()
        x_t = x_d.rearrange("(c s) d -> c s d", s=NT)
        out_t = out_d.rearrange("(c s) d -> c s d", s=NT)

        ps = [stack.enter_context(nc.psum_tensor("ps%d" % i, [128, 512], F32)) for i in range(8)]
        psrr = [0]

        def next_ps():
            i = psrr[0] % 8
            psrr[0] += 1
            return i

        BIG1 = p.sb([128, 32768], BF16, "big1")
        MS = p.sb([128, 65536], BF16, "ms")
        BIG2 = MS[:, 0:32768]
        ident = p.sb([128, 128], F32, "ident")
        p.dma("sp", ident[:], ident_d, "const", writes=["ident"])
        ones = p.sb([128, 128], F32, "ones")
        p.op("dve", lambda e: e.memset(ones[:], 1.0), writes=["ones"])

        c_col = p.sb([128, KT], F32, "c_col")
        p.dma("sp", c_col[:], c_d, "const", writes=["c_col"])
        c_act = p.sb([128, KT], F32, "c_act")
        p.op("act", lambda e: e.activation(out=c_act[:], in_=c_col[:], func=AF.Silu),
             reads=["c_col"], writes=["c_act"])
        c_actb = p.sb([128, KT], BF16, "c_actb")
        p.op("dve", lambda e: e.tensor_copy(out=c_actb[:], in_=c_act[:]),
             reads=["c_act"], writes=["c_actb"])
        c_rep = p.sb([128, KT, 128], BF16, "c_rep")
        for kt in range(KT):
            p.op("dve", lambda e, kt=kt: e.tensor_scalar(
                out=c_rep[:, kt, :], in0=ones[:], scalar1=c_act[:, kt:kt + 1], scalar2=None,
                op0=ALU.mult), reads=["c_act", "ones"], writes=[("c_rep", kt)])

        adab_fm = p.sb([128, 96], F32, "adab_fm")
        p.dma("sp", adab_fm[:], adab_fm_d, "const", writes=["adab_fm"])
        g1 = p.sb([128, KT], F32, "g1")
        g2 = p.sb([128, KT], F32, "g2")
        p.dma("sp", g1[:], g1_d, "const", writes=["g1"])
        p.dma("sp", g2[:], g2_d, "const", writes=["g2"])

        mod_fm = p.sb([128, 96], F32, "mod_fm")
        p.op("dve", lambda e: e.memset(mod_fm[:], 0.0), writes=[("mod", i) for i in range(12)])
        gate_bc = MS[:, 57344:65536].bitcast(F32)
        gate_d = nc.dram_tensor("gate_scratch", [128, 2 * D], F32, kind="Internal").ap()
        p.dma("sp", gate_bc, adab_bc_d, "const", writes=[("gate_bc", i) for i in range(8)])
        ACH = 1024
        aw = [BIG2[:, i * 16384:(i + 1) * 16384].rearrange("p (a b) -> p a b", a=KT) for i in range(2)]

        def ada_chunk(ci):
            b = ci % 2
            n0 = ci * ACH
            p.dma("pool", aw[b], adaw_d[:, ci], "aw%d" % b, writes=[("aw", b)], max_dma_last_dim=4096)
            sec = n0 // D
            if sec in (2, 5):
                for h in range(ACH // 512):
                    pi = next_ps()
                    for kt in range(KT):
                        p.op("pe", lambda e, kt=kt, pi=pi, h=h: e.matmul(
                            ps[pi][:, :], lhsT=c_rep[:, kt, :], rhs=aw[b][:, kt, h * 512:(h + 1) * 512],
                            start=(kt == 0), stop=(kt == KT - 1)),
                            reads=[("aw", b), ("c_rep", kt)], writes=[("ps", pi)])
                    g0 = (0 if sec == 2 else D) + (n0 % D) + h * 512
                    p.op("dve", lambda e, pi=pi, g0=g0: e.tensor_tensor(
                        out=gate_bc[:, g0:g0 + 512], in0=ps[pi][:, :], in1=gate_bc[:, g0:g0 + 512],
                        op=ALU.add), reads=[("ps", pi), ("gate_bc", g0 // 512)], writes=[("gate_bc", g0 // 512)])
            else:
                pi = next_ps()
                for jj in range(ACH // 128):
                    j = n0 // 128 + jj
                    for kt in range(KT):
                        p.op("pe", lambda e, kt=kt, pi=pi, jj=jj: e.matmul(
                            ps[pi][:, jj:jj + 1], lhsT=aw[b][:, kt, jj * 128:(jj + 1) * 128],
                            rhs=c_actb[:, kt:kt + 1], start=(kt == 0), stop=(kt == KT - 1)),
                            reads=[("aw", b), "c_actb"], writes=[("ps", pi)])
                j0 = n0 // 128
                nj = ACH // 128
                p.op("dve", lambda e, pi=pi, j0=j0, nj=nj: e.tensor_tensor(
                    out=mod_fm[:, j0:j0 + nj], in0=ps[pi][:, 0:nj], in1=adab_fm[:, j0:j0 + nj],
                    op=ALU.add), reads=[("ps", pi), "adab_fm"], writes=[("mod", j0 // 8)])

        A1 = p.sb([128, KT], F32, "A1")
        A2 = p.sb([128, KT], F32, "A2")

        def make_A(A, g, gname, sec_scale):
            j0 = sec_scale * 16
            p.op("dve", lambda e: e.scalar_tensor_tensor(
                out=A[:], in0=mod_fm[:, j0:j0 + 16], scalar=1.0, in1=g[:],
                op0=ALU.add, op1=ALU.mult),
                reads=[("mod", j0 // 8), ("mod", j0 // 8 + 1), gname], writes=[("A", sec_scale)])

        for ci in range(4):
            ada_chunk(ci)
        make_A(A1, g1, "g1", 1)
        if debug == "A":
            dbg_mod = nc.dram_tensor("dbg_mod", [128, 32], F32, kind="ExternalOutput").ap()
            p.dma("sp", dbg_mod, mod_fm[:, 0:32], "dbg", reads=[("mod", i) for i in range(4)])
            dbg_A = nc.dram_tensor("dbg_A", [128, 16], F32, kind="ExternalOutput").ap()
            p.dma("sp", dbg_A, A1[:], "dbg", reads=[("A", 1)])
            p.barrier()
            p.emit()
            return nc

        hT = BIG1[:].rearrange("p (a b) -> p a b", a=KT)
        mixT = BIG2.rearrange("p (a b) -> p a b", a=KT)
        xb = [MS[:, 32768 + i * 4096:32768 + (i + 1) * 4096].bitcast(F32) for i in range(4)]
        xn = MS[:, 49152:53248].bitcast(F32)
        junk = xn
        ssq = p.sb([128, 2 * NT], F32, "ssq")
        rstd = p.sb([128, 2 * NT], F32, "rstd")

        def rms_stats(src_tile, src_key, si):
            p.op("act", lambda e: e.activation(out=junk, in_=src_tile, func=AF.Square,
                                               accum_out=ssq[:, si:si + 1]),
                 reads=[src_key], writes=["xn", ("ssq", si)])
            p.op("act", lambda e: e.activation(out=rstd[:, si:si + 1], in_=ssq[:, si:si + 1], func=AF.Sqrt,
                                               scale=1.0 / D, bias=EPS),
                 reads=[("ssq", si)], writes=[("rstd", si)])
            p.op("dve", lambda e: e.reciprocal(out=rstd[:, si:si + 1], in_=rstd[:, si:si + 1]),
                 reads=[("rstd", si)], writes=[("rstd", si)])

        def norm_transpose(src_tile, src_key, si, A, Akey, shift_j0, dst, c0, dkey):
            rms_stats(src_tile, src_key, si)
            p.op("act", lambda e: e.activation(out=xn, in_=src_tile, func=AF.Copy,
                                               scale=rstd[:, si:si + 1]),
                 reads=[src_key, ("rstd", si)], writes=["xn"])
            mk = [("mod", shift_j0 // 8), ("mod", shift_j0 // 8 + 1)]
            for q4 in range(KT // 4):
                pi = next_ps()
                for i in range(4):
                    kt = q4 * 4 + i
                    p.op("pe", lambda e, kt=kt, i=i, pi=pi: e.transpose(
                        out=ps[pi][:, i * 128:(i + 1) * 128], in_=xn[:, kt * 128:(kt + 1) * 128],
                        identity=ident[:]), reads=["xn", "ident"], writes=[("ps", pi)])
                for i in range(4):
                    kt = q4 * 4 + i
                    if i % 2 == 0:
                        p.op("dve", lambda e, kt=kt, i=i, pi=pi: e.tensor_scalar(
                            out=dst[:, kt, c0:c0 + 128], in0=ps[pi][:, i * 128:(i + 1) * 128],
                            scalar1=A[:, kt:kt + 1], scalar2=mod_fm[:, shift_j0 + kt:shift_j0 + kt + 1],
                            op0=ALU.mult, op1=ALU.add),
                            reads=[("ps", pi), Akey] + mk, writes=[dkey])
                    else:
                        p.op("act", lambda e, kt=kt, i=i, pi=pi: e.activation(
                            out=dst[:, kt, c0:c0 + 128], in_=ps[pi][:, i * 128:(i + 1) * 128],
                            func=AF.Identity, scale=A[:, kt:kt + 1],
                            bias=mod_fm[:, shift_j0 + kt:shift_j0 + kt + 1]),
                            reads=[("ps", pi), Akey] + mk, writes=[dkey])

        for s in range(NT):
            b = s % 4
            p.dma("sp", xb[b], x_t[:, s, :], "xb%d" % b, writes=[("xb", b)])
            norm_transpose(xb[b], ("xb", b), s, A1, ("A", 1), 0, hT, s * 128, ("hT", s))

        for ci in range(4, 12):
            ada_chunk(ci)
        make_A(A2, g2, "g2", 4)

        if debug == "B":
            dbg_hT = nc.dram_tensor("dbg_hT", [128, KT, S], BF16, kind="ExternalOutput").ap()
            dbg_mod = nc.dram_tensor("dbg_mod", [128, 96], F32, kind="ExternalOutput").ap()
            dbg_gate = nc.dram_tensor("dbg_gate", [128, 2 * D], F32, kind="ExternalOutput").ap()
            p.dma("sp", dbg_hT, hT, "dbg", reads=[("hT", s) for s in range(NT)])
            p.dma("sp", dbg_mod, mod_fm[:], "dbg", reads=[("mod", i) for i in range(12)])
            p.dma("sp", dbg_gate, gate_bc, "dbg", reads=[("gate_bc", i) for i in range(8)])
            p.barrier()
            p.emit()
            return nc

        mixT_d = nc.dram_tensor("mixT_scratch", [128, KT, S], BF16, kind="Internal").ap()
        p.dma("sp", gate_d, gate_bc, "gatew", reads=[("gate_bc", i) for i in range(8)], writes=["gated"])
        p.barrier()
        w_in_d = din("w_in_h", [128, 36, KT, 128])
        if debug == "T":
            p.dma("sp", mixT_d, hT, "mixw", reads=[("hT", s) for s in range(NT)], writes=["mixd"])
        else:
            mixer_phase(p, nc, locals())
        p.barrier()
        if debug in ("S5", "RK"):
            dbg_mix = nc.dram_tensor("dbg_mix", [128, 8, S], BF16, kind="ExternalOutput").ap()
            k0 = 0 if debug == "S5" else 8
            p.dma("sp", dbg_mix, mixT_d[:, k0:k0 + 8, :], "dbg")
            p.barrier()
            p.emit()
            return nc

        w_out_d = din("w_out", [128, 4, KT, 512])
        x1_d = nc.dram_tensor("x1_scratch", [S, D], F32, kind="Internal").ap()
        x1_t = x1_d.rearrange("(c s) d -> c s d", s=NT)
        wo = [BIG1[:, i * 8192:(i + 1) * 8192].rearrange("p (a b) -> p a b", a=KT) for i in range(2)]
        p.dma("sp", mixT, mixT_d, "mixld", reads=["mixd"], writes=["mixT"])
        p.dma("sp", gate_bc, gate_d, "gateld", reads=["gated"], writes=["gate_bc"])
        xq = [xb[i][:, 0:512] for i in range(4)]
        tq = [xb[i][:, 512:1024] for i in range(4)]
        for nch in range(4):
            b = nch % 2
            cols = slice(nch * 512, (nch + 1) * 512)
            p.dma("pool", wo[b], w_out_d[:, nch], "wo%d" % b, writes=[("wo", b)], max_dma_last_dim=4096)
            for s in range(NT):
                pi = next_ps()
                r = s % 4
                for kt in range(KT):
                    p.op("pe", lambda e, kt=kt, pi=pi, s=s, b=b: e.matmul(
                        ps[pi][:, :], lhsT=mixT[:, kt, s * 128:(s + 1) * 128], rhs=wo[b][:, kt, :],
                        start=(kt == 0), stop=(kt == KT - 1)),
                        reads=[("wo", b), "mixT"], writes=[("ps", pi)])
                p.dma("sp", xq[r], x_t[:, s, cols], "xq%d" % r, writes=[("xq", r)])
                p.op("dve", lambda e, pi=pi, r=r, cols=cols: e.tensor_tensor(
                    out=tq[r], in0=ps[pi][:, :], in1=gate_bc[:, cols], op=ALU.mult),
                    reads=[("ps", pi), "gate_bc"], writes=[("tq", r)])
                p.op("dve", lambda e, r=r: e.tensor_tensor(
                    out=tq[r], in0=tq[r], in1=xq[r], op=ALU.add),
                    reads=[("tq", r), ("xq", r)], writes=[("tq", r)])
                p.dma("sp", x1_t[:, s, cols], tq[r], "x1w%d" % r, reads=[("tq", r)], writes=[("x1d", s)])
        p.barrier()

        w1_d = din("ffn_w1", [128, FFN // 256, KT, 256])
        w2_d = din("ffn_w2", [128, 4, 8, 8, 512])
        fg_d = din("fgain_bc", [128, D])
        fgain = MS[:, 53248:57344].bitcast(F32)
        p.dma("sp", fgain, fg_d, "fgain", writes=["fgain"])
        hidT = BIG1[:].rearrange("p (a b) -> p a b", a=FFN // 128)
        NWB = 6
        wring = [BIG2[:, 0:4096], BIG2[:, 4096:8192], MS[:, 28672:32768],
                 BIG2[:, 8192:12288], BIG2[:, 12288:16384], MS[:, 57344:61440]]
        w1c = [r_.rearrange("p (a b) -> p a b", a=KT) for r_ in wring]
        w2c = [r_.rearrange("p (a b) -> p a b", a=8) for r_ in wring]
        wcnt = [0]
        h2g = BIG2[:, 16384:16384 + KT * 512].rearrange("p (a b) -> p a b", a=KT)
        rl = [BIG2[:, 24576 + i * 1024:24576 + (i + 1) * 1024].bitcast(F32) for i in range(2)]
        tq2 = [BIG2[:, 26624 + i * 1024:26624 + (i + 1) * 1024].bitcast(F32) for i in range(2)]
        HW1 = 256
        for G in range(4):
            for i in range(4):
                s = 4 * G + i
                p.dma("sp", xb[i], x1_t[:, s, :], "xb%d" % i, reads=[("x1d", s)], writes=[("xb", i)])
                norm_transpose(xb[i], ("xb", i), NT + s, A2, ("A", 4), 48, h2g, i * 128, "h2g")
            for hc in range(FFN // HW1):
                b = wcnt[0] % NWB
                wcnt[0] += 1
                p.dma("pool", w1c[b], w1_d[:, hc], "wb%d" % b, writes=[("wb", b)], max_dma_last_dim=4096)
                for j in range(HW1 // 128):
                    ht = hc * (HW1 // 128) + j
                    pi = next_ps()
                    for kt in range(KT):
                        p.op("pe", lambda e, kt=kt, pi=pi, j=j, b=b: e.matmul(
                            ps[pi][:, :], lhsT=w1c[b][:, kt, j * 128:(j + 1) * 128], rhs=h2g[:, kt, :],
                            start=(kt == 0), stop=(kt == KT - 1)),
                            reads=[("wb", b), "h2g"], writes=[("ps", pi)])
                    rb = ht % 2
                    p.op("act", lambda e, pi=pi, rb=rb: e.activation(out=rl[rb], in_=ps[pi][:, :], func=AF.Relu),
                         reads=[("ps", pi)], writes=[("rl", rb)])
                    p.op("dve", lambda e, pi=pi, rb=rb, ht=ht: e.tensor_tensor(
                        out=hidT[:, ht, :], in0=ps[pi][:, :], in1=rl[rb], op=ALU.mult),
                        reads=[("ps", pi), ("rl", rb)], writes=[("hidT", ht)])
            for nch in range(4):
                cols = slice(nch * 512, (nch + 1) * 512)
                gcols = slice(D + nch * 512, D + (nch + 1) * 512)
                pis = [next_ps() for i in range(4)]
                for hg in range(8):
                    b = wcnt[0] % NWB
                    wcnt[0] += 1
                    p.dma("pool", w2c[b], w2_d[:, nch, hg], "wb%d" % b, writes=[("wb", b)], max_dma_last_dim=4096)
                    for i in range(4):
                        for h8 in range(8):
                            ht = hg * 8 + h8
                            p.op("pe", lambda e, i=i, h8=h8, ht=ht, b=b, pis=pis: e.matmul(
                                ps[pis[i]][:, :], lhsT=hidT[:, ht, i * 128:(i + 1) * 128], rhs=w2c[b][:, h8, :],
                                start=(ht == 0), stop=(ht == 63)),
                                reads=[("wb", b), ("hidT", ht)], writes=[("ps", pis[i])])
                for i in range(4):
                    tb = i % 2
                    p.op("dve", lambda e, i=i, tb=tb, pis=pis, gcols=gcols: e.tensor_tensor(
                        out=tq2[tb], in0=ps[pis[i]][:, :], in1=gate_bc[:, gcols], op=ALU.mult),
                        reads=[("ps", pis[i])], writes=[("tq2", tb)])
                    p.op("dve", lambda e, i=i, tb=tb, cols=cols: e.tensor_tensor(
                        out=xb[i][:, cols], in0=xb[i][:, cols], in1=tq2[tb], op=ALU.add),
                        reads=[("tq2", tb), ("xb", i)], writes=[("xb", i)])
            for i in range(4):
                s = 4 * G + i
                rms_stats(xb[i], ("xb", i), s)
                p.op("act", lambda e, i=i, s=s: e.activation(out=xn, in_=xb[i], func=AF.Copy,
                                                             scale=rstd[:, s:s + 1]),
                     reads=[("xb", i), ("rstd", s)], writes=["xn"])
                p.op("dve", lambda e, i=i: e.tensor_tensor(out=xb[i], in0=xn, in1=fgain, op=ALU.mult),
                     reads=["xn", "fgain"], writes=[("xb", i)])
                p.dma("sp", out_t[:, s, :], xb[i], "outw%d" % i, reads=[("xb", i)], writes=[("outd", s)])
        p.barrier()
        p.emit()
    return nc


I32 = mybir.dt.int32
TWO_PI = 6.283185307179586


class Arena:
    def __init__(self, MS, nbytes):
        self.MS = MS
        self.cap = nbytes
        self.off = 0

    def reset(self, off=0):
        self.off = off

    def alloc(self, parts, shape, dtype):
        n = 1
        for d_ in shape:
            n *= d_
        esz = 2 if dtype == BF16 else 4
        nb = (n * esz + 31) // 32 * 32
        o = self.off
        self.last = o // 2
        self.off += nb
        assert self.off <= self.cap, ("arena overflow", self.off, self.cap)
        ap = self.MS[0:parts, o // 2:(o + n * esz) // 2]
        if dtype != BF16:
            ap = ap.bitcast(dtype)
        if len(shape) > 1:
            names = " ".join("d%d" % i for i in range(len(shape)))
            kw = {"d%d" % i: shape[i] for i in range(1, len(shape))}
            ap = ap.rearrange("p (%s) -> p %s" % (names, names), **kw)
        return ap


def run_gens(gens):
    gens = list(gens)
    while gens:
        alive = []
        for g_ in gens:
            try:
                next(g_)
                alive.append(g_)
            except StopIteration:
                pass
        gens = alive


def bc(ap, axis, n):
    v = ap.unsqueeze(axis)
    shp = list(v.shape)
    shp[axis] = n
    return v.broadcast_to(shp)


def mixer_phase(p, nc, env):
    if env.get("debug") != "S5":
        rwkv_phase(p, nc, env)
        p.barrier()
    if env.get("debug") != "RK":
        s5_phase(p, nc, env)


CDEC = -0.6065306597126334
GN_EPS = 64e-5


def rwkv_phase(p, nc, env):
    ps, next_ps, hT, MS, din, ident = (env[k] for k in ("ps", "next_ps", "hT", "MS", "din", "ident"))
    mixT_d = env["mixT_d"]
    w_in_d = env["w_in_d"]
    ar = Arena(MS, 131072)

    prm_d = din("rk_prm", [128, 112])
    wup_d = din("rk_wup", [128, 1024])
    aup_d = din("rk_aup", [128, 1024])
    gup1_d = din("rk_gup1", [128, 1024])
    gup2_d = din("rk_gup2", [32, 1024])
    lng_d = din("rk_lng", [8, 128, 64])
    lnb_d = din("rk_lnb", [8, 128, 64])
    smask_d = din("rk_smask", [128, 512])
    mask5_d = din("rk_mask5", [128, 2, 320])
    i64_d = din("rk_i64", [128, 64])
    bones_d = din("rk_bones", [128, 128])

    def tt(eng, out, a, b, op, r, w):
        p.op(eng, lambda e: e.tensor_tensor(out=out, in0=a, in1=b, op=op), reads=r, writes=w)

    def ts(eng, out, a, s1, s2, op0, op1, r, w):
        if op1 is None:
            p.op(eng, lambda e: e.tensor_scalar(out=out, in0=a, scalar1=s1, scalar2=None, op0=op0), reads=r, writes=w)
        else:
            p.op(eng, lambda e: e.tensor_scalar(out=out, in0=a, scalar1=s1, scalar2=s2, op0=op0, op1=op1),
                 reads=r, writes=w)

    def stt(eng, out, a, sc, b, op0, op1, r, w):
        p.op(eng, lambda e: e.scalar_tensor_tensor(out=out, in0=a, scalar=sc, in1=b, op0=op0, op1=op1),
             reads=r, writes=w)

    def act(out, in_, func, r, w, **kw):
        p.op("act", lambda e: e.activation(out=out, in_=in_, func=func, **kw), reads=r, writes=w)

    def cp(eng, out, in_, r, w):
        if eng == "act":
            act(out, in_, AF.Copy, r, w)
        else:
            p.op(eng, lambda e: e.tensor_copy(out=out, in_=in_), reads=r, writes=w)

    def mm(out, lhsT, rhs, start, stop, r, w):
        p.op("pe", lambda e: e.matmul(out, lhsT=lhsT, rhs=rhs, start=start, stop=stop), reads=r, writes=w)

    prm = ar.alloc(128, [112], F32)
    c0 = ar.alloc(128, [28], F32)
    omk = ar.alloc(128, [8], F32)
    wup = ar.alloc(128, [1024], BF16)
    aup = ar.alloc(128, [1024], BF16)
    gup1 = ar.alloc(128, [1024], BF16)
    gup2 = ar.alloc(128, [1024], BF16)
    smask = ar.alloc(128, [512], BF16)
    mask5 = ar.alloc(128, [2, 320], F32)
    i64b = ar.alloc(128, [64], BF16)
    bones = ar.alloc(128, [128], F32)
    tw = ar.alloc(128, [S], BF16)
    ad = ar.alloc(128, [S], BF16)
    sg1 = ar.alloc(128, [S], BF16)
    sg2 = ar.alloc(128, [S], BF16)
    p.dma("sp", prm, prm_d, "rkc0", writes=["prm"])
    p.dma("sp", mask5, mask5_d, "rkc1", writes=["mask5"])
    p.dma("sp", bones, bones_d, "rkc2", writes=["bones"])
    p.dma("pool", wup, wup_d, "rkc3", writes=["wup"])
    p.dma("pool", aup, aup_d, "rkc4", writes=["aup"])
    p.dma("pool", gup1, gup1_d, "rkc5", writes=["gup1"])
    p.dma("pool", gup2[0:32, :], gup2_d, "rkc6", writes=["gup2"])
    p.dma("pool", smask, smask_d, "rkc7", writes=["smask"])
    p.dma("pool", i64b, i64_d, "rkc8", writes=["i64b"])
    MUP, MUN, W0, A0, KK, KA, RK = 0, 28, 56, 72, 88, 96, 104
    tt("dve", c0, prm[:, MUP:MUP + 28], prm[:, MUN:MUN + 28], ALU.add, ["prm"], ["c0"])
    ts("dve", c0, c0, -1.0, 1.0, ALU.mult, ALU.add, ["c0"], ["c0"])
    ts("dve", omk, prm[:, KA:KA + 8], -1.0, 1.0, ALU.mult, ALU.add, ["prm"], ["omk"])

    pn = ar.alloc(128, [S + 2], F32)
    wic = ar.alloc(128, [KT, 128], BF16)
    rS = ar.alloc(128, [S], F32)
    rS_off = ar.last
    kS = ar.alloc(128, [S], F32)
    assert ar.last == rS_off + 4096
    kap = ar.alloc(128, [S], F32)
    v_bf = ar.alloc(128, [S], BF16)
    g_bf = ar.alloc(128, [S], BF16)
    tl = [[None] * 4 for _ in range(2)]
    tl_off = [[0] * 4 for _ in range(2)]
    QR = [None, None]
    for d_ in range(2):
        QR[d_] = ar.alloc(128, [32, 2, 64], BF16)
        qr_off = ar.last
        tl_off[d_][0] = qr_off
        for a_ in (1, 2):
            tl[d_][a_] = ar.alloc(128, [S], BF16)
            tl_off[d_][a_] = ar.last
    tmp_off = tl_off[1][0]
    ynb_flat = [MS[:, tl_off[0][0]:tl_off[0][0] + 2048]]
    Gam = ar.alloc(128, [2, 32], F32)
    QW = 512
    NQ = S // QW
    CH = QW // 64
    Tset = [[ar.alloc(128, [QW], F32) for _ in range(4)], None]
    TE = ar.alloc(128, [QW], F32)
    tot8 = ar.alloc(128, [2, CH], F32)

    def run_interleaved0(gens):
        gens = list(gens)
        while gens:
            alive = []
            for g_ in gens:
                try:
                    next(g_)
                    alive.append(g_)
                except StopIteration:
                    pass
            gens = alive
    lng = ar.alloc(128, [64], F32)
    lnb = ar.alloc(128, [64], F32)
    st = ar.alloc(128, [4, 32], F32)
    CM = [ar.alloc(128, [2, 320], BF16)]
    cm_off = ar.last
    CM.append(ar.alloc(128, [2, 320], BF16))
    TT = [ar.alloc(128, [2, 192], BF16) for _ in range(2)]
    LINV = [ar.alloc(128, [2, 64], BF16) for _ in range(2)]
    IBS = [[ar.alloc(128, [2, 192], F32) for _ in range(2)]]
    pnb = pn.bitcast(BF16)
    o_ = 0
    for _ in range(2):
        CM.append(pnb[:, o_:o_ + 640].rearrange("p (d q) -> p d q", d=2)); o_ += 640
    for _ in range(2):
        TT.append(pnb[:, o_:o_ + 384].rearrange("p (d q) -> p d q", d=2)); o_ += 384
    for _ in range(2):
        LINV.append(pnb[:, o_:o_ + 128].rearrange("p (d q) -> p d q", d=2)); o_ += 128
    ib2 = []
    for _ in range(2):
        ib2.append(pnb[:, o_:o_ + 768].bitcast(F32).rearrange("p (d q) -> p d q", d=2)); o_ += 768
    IBS.append(ib2)
    assert o_ <= 4100
    Xn = ar.alloc(128, [2, 64], BF16)
    Ub = ar.alloc(128, [2, 64], BF16)
    H32 = ar.alloc(128, [2, 64], F32)
    Hb = ar.alloc(128, [2, 64], BF16)
    tmpH = ar.alloc(128, [2, 64], F32)
    mixt = tl[1][1]
    assert ar.off // 2 - cm_off >= 4 * QW * 2
    Tset[1] = [MS[:, cm_off + i_ * 2 * QW:cm_off + (i_ + 1) * 2 * QW].bitcast(F32) for i_ in range(4)]
    set1k = [("TA", 1), ("TB", 1), ("TC", 1), ("TD", 1)]
    smallk = [(n_, hp_) for n_ in ("Xn", "U", "H", "Hb", "tmpH") for hp_ in range(2)]

    p.op("pool", lambda e: e.memset(pn[:, 0:1], 0.0), writes=["pn"])
    p.op("pool", lambda e: e.memset(pn[:, S + 1:S + 2], 0.0), writes=["pn"])

    def proj_fm(col0, ncols, tidx, dst, dkey):
        p.dma("pool", wic, w_in_d[:, col0 // 128], "rkwic", writes=["wic"], max_dma_last_dim=4096)
        pn_nat = pn[0:ncols, 1:S + 1].rearrange("p (c s) -> p s c", s=16)
        for tb in range(4):
            pi = next_ps()
            for kt in range(KT):
                mm(ps[pi][0:ncols, :], wic[:, kt, 0:ncols], hT[:, kt, tb * 512:(tb + 1) * 512],
                   kt == 0, kt == KT - 1, ["wic"], [("ps", pi)])
            act(pn_nat[:, tb * 4:(tb + 1) * 4, :], ps[pi][0:ncols, :].rearrange("p (s c) -> p s c", s=4), AF.Copy,
                [("ps", pi)], ["pn"])
        act(dst[0:ncols, :], pn[0:ncols, 1:S + 1], AF.Copy, ["pn", "c0"], [dkey], scale=c0[0:ncols, tidx:tidx + 1])
        stt("dve", dst[0:ncols, :], pn[0:ncols, 0:S], prm[0:ncols, MUP + tidx:MUP + tidx + 1], dst[0:ncols, :],
            ALU.mult, ALU.add, ["pn", "prm", dkey], [dkey])
        stt("dve", dst[0:ncols, :], pn[0:ncols, 2:S + 2], prm[0:ncols, MUN + tidx:MUN + tidx + 1], dst[0:ncols, :],
            ALU.mult, ALU.add, ["pn", "prm", dkey], [dkey])

    proj_fm(4096, 128, 24, rS, "rS")
    act(tw, rS, AF.Tanh, ["rS"], ["tw"])
    proj_fm(4224, 128, 25, rS, "rS")
    cp("act", ad, rS, ["rS"], ["ad"])
    proj_fm(4352, 128, 26, rS, "rS")
    act(sg1, rS, AF.Sigmoid, ["rS"], ["sg1"])
    proj_fm(4480, 32, 27, rS, "rS")
    act(sg2[0:32, :], rS[0:32, :], AF.Sigmoid, ["rS"], ["sg2"])

    yT = [None]

    for j in range(8):
        js = slice(j * 128, (j + 1) * 128)
        p.dma("sp", lng, lng_d[j], "rklng", writes=["lng"])
        p.dma("sp", lnb, lnb_d[j], "rklnb", writes=["lnb"])
        proj_fm(1024 + j * 128, 128, j, rS, "rS")
        proj_fm(2048 + j * 128, 128, 8 + j, kS, "kS")
        proj_fm(3072 + j * 128, 128, 16 + j, kap, "kap")
        cp("act", v_bf, kap, ["kap"], ["v_bf"])
        kapk = ["kap"] + [("kap", q_) for q_ in range(NQ)]
        ts("dve", kap, kS, prm[:, KK + j:KK + j + 1], None, ALU.mult, None, ["kS", "prm", "v_bf"], kapk)
        pending = [None]
        for q in range(NQ):
            qs = slice(q * QW, (q + 1) * QW)
            TA0, TB0 = Tset[0][0], Tset[0][1]
            tt("pool", TA0, kap[:, qs], kap[:, qs], ALU.mult, [("kap", q)], [("TA", 0)])
            pi = next_ps()
            mm(ps[pi][:, 0:QW], bones, TA0, True, True, ["bones", ("TA", 0)], [("ps", pi)])
            act(TB0, ps[pi][:, 0:QW], AF.Sqrt, [("ps", pi)], [("TB", 0)])
            ts("dve", TB0, TB0, 1e-12, None, ALU.max, None, [("TB", 0)], [("TB", 0)])
            p.op("dve", lambda e, TB0=TB0: e.reciprocal(out=TB0, in_=TB0), reads=[("TB", 0)], writes=[("TB", 0)])
            tt("dve", kap[:, qs], kap[:, qs], TB0, ALU.mult, [("kap", q), ("TB", 0)], [("kap", q)])

            def chain(d, q=q, qs=qs):
                TA, TB, TC, TD = Tset[d]
                kA, kB, kC, kD = ("TA", d), ("TB", d), ("TC", d), ("TD", d)
                ds_ = slice(d * 64, (d + 1) * 64)
                pi = next_ps()
                mm(ps[pi][:, 0:QW], wup[ds_, js], tw[ds_, qs], True, True, ["wup", "tw"], [("ps", pi)])
                act(TA, ps[pi][:, 0:QW], AF.Sigmoid, [("ps", pi), "prm"], [kA],
                    bias=prm[:, W0 + d * 8 + j:W0 + d * 8 + j + 1])
                yield
                pi = next_ps()
                mm(ps[pi][:, 0:QW], aup[ds_, js], ad[ds_, qs], True, True, ["aup", "ad"], [("ps", pi)])
                act(TB, ps[pi][:, 0:QW], AF.Sigmoid, [("ps", pi), "prm"], [kB],
                    bias=prm[:, A0 + d * 8 + j:A0 + d * 8 + j + 1])
                yield
                p.op("dve", lambda e: e.tensor_tensor_scan(out=TD, data0=smask[:, 0:QW], data1=TA, initial=0.0,
                                                            op0=ALU.mult, op1=ALU.add),
                     reads=["smask", kA], writes=[kD])
                yield
                ts("dve", TC, TB, prm[:, KA + j:KA + j + 1], omk[:, j:j + 1], ALU.mult, ALU.add,
                   [kB, "prm", "omk"], [kC])
                yield
                tt("pool", TC, TC, kS[:, qs], ALU.mult, [kC, "kS"], [kC])
                yield
                tt("pool", TB, TB, kap[:, qs], ALU.mult, [kB, ("kap", q)], [kB])
                yield
                TD3 = TD.rearrange("p (c t) -> p c t", t=64)
                TA3 = TA.rearrange("p (c t) -> p c t", t=64)
                if d == 1:
                    cp("dve", tot8[:, d, :], TD3[:, :, 63], [kD], [("tot8", d)])
                    yield
                    tt("dve", TD3, bc(tot8[:, d, :], 2, 64), TD3, ALU.subtract, [("tot8", d), kD], [kD])
                    yield
                    tt("dve", TD, TD, TA, ALU.add, [kD, kA], [kD])
                    yield
                tt("dve", TA, TD, TA, ALU.subtract, [kD, kA], [kA])
                yield
                act(TA, TA, AF.Exp, [kA], [kA], scale=CDEC)
                yield
                tt("dve", QR[d][:, q * CH:(q + 1) * CH, 0, :], kap[:, qs].rearrange("p (c t) -> p c t", t=64),
                   TA3, ALU.mult, [("kap", q), kA], [("tl", d, 0)])
                yield
                act(TA, TD, AF.Exp, [kD], [kA], scale=CDEC)
                yield
                tt("dve", QR[d][:, q * CH:(q + 1) * CH, 1, :], rS[:, qs].rearrange("p (c t) -> p c t", t=64),
                   TA3, ALU.mult, ["rS", kA], [("tl", d, 3)])
                cp("pool", Gam[:, d, q * CH:(q + 1) * CH], TA3[:, :, 63 if d == 0 else 0], [kA], ["Gam"])
                yield
                act(TA, TD, AF.Exp, [kD], [kA], scale=-CDEC)
                yield
                tt("dve", tl[d][1][:, qs], TB, TA, ALU.mult, [kB, kA], [("tl", d, 1)])
                tt("pool", tl[d][2][:, qs], TC, TA, ALU.mult, [kC, kA], [("tl", d, 2)])
                yield

            gens_ = [chain(0), chain(1)]
            if pending[0] is not None:
                gens_.append(pending[0])
            run_interleaved0(gens_)
            tt("pool", TE, Tset[0][2], Tset[1][2], ALU.add, [("TC", 0), ("TC", 1)], ["TE"])

            def misc(q=q, qs=qs):
                tt("dve", TE, rS[:, qs], TE, ALU.mult, ["rS", "TE"], ["TE"])
                yield
                ts("dve", TE, TE, prm[:, RK + j:RK + j + 1], None, ALU.mult, None, ["TE", "prm"], ["TE"])
                yield
                pi = next_ps()
                mm(ps[pi][:, 0:QW], bones, TE, True, True, ["bones", "TE"], [("ps", pi)])
                tt("dve", kap[:, qs], ps[pi][:, 0:QW], v_bf[:, qs], ALU.mult,
                   [("ps", pi), "v_bf", ("kap", q)], [("kap", q)])
                yield
                pi = next_ps()
                mm(ps[pi][:, 0:QW], gup1[:, js], sg1[:, qs], True, False, ["gup1", "sg1"], [("ps", pi)])
                mm(ps[pi][:, 0:QW], gup2[0:32, js], sg2[0:32, qs], False, True, ["gup2", "sg2"], [("ps", pi)])
                cp("act", g_bf[:, qs], ps[pi][:, 0:QW], [("ps", pi)], ["g_bf"])
                yield

            pending[0] = misc()
        run_interleaved0([pending[0]])
        pending[0] = None
        yTMv = MS[:, rS_off:rS_off + 8192].bitcast(F32).rearrange("p (d n v) -> p d n v", d=2, n=32)
        p.op("pool", lambda e: e.memset(H32.rearrange("p a b -> p (a b)"), 0.0), reads=set1k + ["TE"],
             writes=[("H", 0), ("H", 1)])
        p.op("pool", lambda e: e.memset(Hb.rearrange("p a b -> p (a b)"), 0.0), reads=set1k + ["TE"],
             writes=[("Hb", 0), ("Hb", 1)])
        tlk = [("tl", d_, a_) for d_ in range(2) for a_ in range(4)]

        def stage_P(i, hp):
            par = i % 4
            IB = IBS[i % 2]
            ibk = i % 2
            pb = hp * 64
            fs = slice(pb, pb + 64)
            nn = [i, 31 - i]
            for d in range(2):
                t_ = slice(nn[d] * 64, nn[d] * 64 + 64)
                Bt, Kt = tl[d][1][fs, t_], tl[d][2][fs, t_]
                Q = QR[d][fs, nn[d], 0, :]
                QRf = QR[d][fs, nn[d]].rearrange("p a t -> p (a t)")
                pi = next_ps()
                mm(ps[pi][fs, 0:128], Bt, QRf, True, True, tlk, [("ps", pi)])
                mm(ps[pi][fs, 128:256], Kt, QRf, True, True, tlk, [("ps", pi)])
                mm(ps[pi][fs, 256:320], Q, Bt, True, True, tlk, [("ps", pi)])
                tt("dve", CM[par][fs, d, :], ps[pi][fs, 0:320], mask5[fs, d, :], ALU.mult,
                   [("ps", pi), "mask5"], [("CM", par, hp)])
            pi = next_ps()
            pbv = ps[pi][:, :].bitcast(BF16)
            for d in range(2):
                t_ = slice(nn[d] * 64, nn[d] * 64 + 64)
                for c_, src in enumerate((tl[d][1][fs, t_], tl[d][2][fs, t_], v_bf[fs, t_])):
                    o = (d * 3 + c_) * 64
                    p.op("pe", lambda e, o=o, src=src, pbv=pbv, fs=fs: e.transpose(
                        out=pbv[fs, o:o + 64], in_=src, identity=i64b[fs, :]),
                        reads=tlk + ["v_bf", "i64b"], writes=[("ps", pi)])
            cp("act", TT[par][fs, :, :], pbv[fs, 0:384].rearrange("p (d q) -> p d q", d=2),
               [("ps", pi)], [("TT", par, hp)])
            tt("pool", IB[0][fs, :, 0:64], CM[par][fs, :, 0:64], bc(i64b[fs, :], 1, 2), ALU.add,
               [("CM", par, hp), "i64b"], [("IB", ibk, 0, hp)])
            cp("pool", IB[0][fs, :, 64:128], CM[par][fs, :, 0:64], [("CM", par, hp)], [("IB", ibk, 0, hp)])
            cp("pool", IB[0][fs, :, 128:192], CM[par][fs, :, 256:320], [("CM", par, hp)], [("IB", ibk, 0, hp)])
            yield
            cur = 0
            for lev in range(6):
                nxt = 1 - cur
                pi = next_ps()
                pv = ps[pi][fs, 0:384].rearrange("p (d q) -> p d q", d=2)
                for d in range(2):
                    o = d * 192
                    P_, M_, MT_ = IB[cur][fs, d, 0:64], IB[cur][fs, d, 64:128], IB[cur][fs, d, 128:192]
                    if lev == 0:
                        mm(ps[pi][fs, o + 64:o + 128], MT_, M_, True, True, [("IB", ibk, cur, hp)], [("ps", pi)])
                        mm(ps[pi][fs, o + 128:o + 192], M_, MT_, True, True, [("IB", ibk, cur, hp)], [("ps", pi)])
                    elif lev < 5:
                        mm(ps[pi][fs, o:o + 128], MT_, IB[cur][fs, d, 0:128], True, True, [("IB", ibk, cur, hp)], [("ps", pi)])
                        mm(ps[pi][fs, o + 128:o + 192], M_, MT_, True, True, [("IB", ibk, cur, hp)], [("ps", pi)])
                    else:
                        mm(ps[pi][fs, o:o + 64], MT_, P_, True, True, [("IB", ibk, cur, hp)], [("ps", pi)])
                if lev == 0:
                    cp("pool", IB[nxt][fs, :, 0:64], IB[cur][fs, :, 0:64], [("IB", ibk, cur, hp)], [("IB", ibk, nxt, hp)])
                    cp("act", IB[nxt][fs, :, 64:192], pv[:, :, 64:192], [("ps", pi)], [("IB", ibk, nxt, hp)])
                elif lev < 5:
                    tt("dve", IB[nxt][fs, :, 0:64], IB[cur][fs, :, 0:64], pv[:, :, 0:64], ALU.add,
                       [("IB", ibk, cur, hp), ("ps", pi)], [("IB", ibk, nxt, hp)])
                    cp("act", IB[nxt][fs, :, 64:192], pv[:, :, 64:192], [("ps", pi)], [("IB", ibk, nxt, hp)])
                else:
                    tt("dve", LINV[par][fs, :, :], IB[cur][fs, :, 0:64], pv[:, :, 0:64], ALU.add,
                       [("IB", ibk, cur, hp), ("ps", pi)], [("LINV", par, hp)])
                cur = nxt
                yield

        def stage_C(i, hp):
            par = i % 4
            pb = hp * 64
            fs = slice(pb, pb + 64)
            nn = [i, 31 - i]
            tsl = [slice(n_ * 64, n_ * 64 + 64) for n_ in nn]
            kCM, kTT, kLI = ("CM", par, hp), ("TT", par, hp), ("LINV", par, hp)
            pi = next_ps()
            for d in range(2):
                o = ps[pi][fs, d * 64:(d + 1) * 64]
                mm(o, QR[d][fs, nn[d], 0, :], Hb[fs, d, :], True, False, tlk + [("Hb", hp)], [("ps", pi)])
                mm(o, CM[par][fs, d, 128:192], TT[par][fs, d, 128:192], False, True, [kCM, kTT], [("ps", pi)])
            act(Xn[fs, :, :], ps[pi][fs, 0:128].rearrange("p (d q) -> p d q", d=2), AF.Copy,
                [("ps", pi)], [("Xn", hp)], scale=-1.0)
            yield
            pi = next_ps()
            for d in range(2):
                mm(ps[pi][fs, d * 64:(d + 1) * 64], LINV[par][fs, d, :], Xn[fs, d, :], True, True,
                   [kLI, ("Xn", hp)], [("ps", pi)])
            cp("dve", Ub[fs, :, :], ps[pi][fs, 0:128].rearrange("p (d q) -> p d q", d=2), [("ps", pi)], [("U", hp)])
            yield
            pi = next_ps()
            for d in range(2):
                o = ps[pi][fs, d * 64:(d + 1) * 64]
                mm(o, QR[d][fs, nn[d], 1, :], Hb[fs, d, :], True, False, tlk + [("Hb", hp)], [("ps", pi)])
                mm(o, CM[par][fs, d, 192:256], TT[par][fs, d, 128:192], False, False, [kCM, kTT], [("ps", pi)])
                mm(o, CM[par][fs, d, 64:128], Ub[fs, d, :], False, True, [kCM, ("U", hp)], [("ps", pi)])
            cp("act", yTMv[fs, 0, nn[0], :], ps[pi][fs, 0:64], [("ps", pi)], [("yTM", hp)])
            cp("dve", yTMv[fs, 1, nn[1], :], ps[pi][fs, 64:128], [("ps", pi)], [("yTM", hp)])
            pi = next_ps()
            for d in range(2):
                o = ps[pi][fs, d * 64:(d + 1) * 64]
                mm(o, TT[par][fs, d, 0:64], Ub[fs, d, :], True, False, [kTT, ("U", hp)], [("ps", pi)])
                mm(o, TT[par][fs, d, 64:128], TT[par][fs, d, 128:192], False, True, [kTT], [("ps", pi)])
            tt("dve", tmpH[fs, :, :], ps[pi][fs, 0:128].rearrange("p (d q) -> p d q", d=2), H32[fs, :, :], ALU.add,
               [("ps", pi), ("H", hp)], [("tmpH", hp)])
            for d in range(2):
                ts("dve" if d == 0 else "pool", H32[fs, d, :], tmpH[fs, d, :], Gam[fs, d, nn[d]:nn[d] + 1], None,
                   ALU.mult, None, [("tmpH", hp), "Gam"], [("H", hp)])
            cp("act", Hb[fs, :, :], H32[fs, :, :], [("H", hp)], [("Hb", hp)])
            yield

        p.op("pool", lambda e: e.memset(yTMv[:, 0, 0, 0:1], 0.0), reads=["rS", "kS"] + tlk,
             writes=["rS", "kS", ("yTM", 0), ("yTM", 1)])
        def run_interleaved(gens):
            gens = list(gens)
            while gens:
                alive = []
                for g_ in gens:
                    try:
                        next(g_)
                        alive.append(g_)
                    except StopIteration:
                        pass
                gens = alive

        def chain_C(i0, hp):
            for i_ in (i0, i0 + 1):
                yield from stage_C(i_, hp)

        allk = [(n_, par_, hp_) for n_ in ("CM", "TT", "LINV") for par_ in range(4) for hp_ in range(2)] + \
               [("IB", a_, b_, hp_) for a_ in range(2) for b_ in range(2) for hp_ in range(2)]
        p.op("pool", lambda e: e.memset(pnb[:, 0:2], 0.0), reads=["pn", "TE"] + set1k,
             writes=allk + ["pn"] + [k_ for k_ in smallk if k_[0] in ("Xn", "U", "tmpH")])
        run_interleaved([stage_P(0, 0), stage_P(0, 1), stage_P(1, 0), stage_P(1, 1)])
        for i in range(0, 32, 2):
            gens = [chain_C(i, 0), chain_C(i, 1)]
            if i + 2 < 32:
                gens += [stage_P(i + 2, 0), stage_P(i + 2, 1), stage_P(i + 3, 0), stage_P(i + 3, 1)]
            run_interleaved(gens)
        p.op("pool", lambda e: e.memset(pn[:, 0:1], 0.0), reads=allk + smallk, writes=["pn"] + allk + set1k)

        y0 = yTMv[:, 0]
        y1 = yTMv[:, 1]
        yk = [("yTM", 0), ("yTM", 1), "rS", "kS"]
        tt("dve", y0, y0, y1, ALU.add, yk, yk)
        p.op("dve", lambda e: e.tensor_reduce(out=st[:, 0, :], in_=y0, axis=AX.X, op=ALU.add), reads=yk, writes=["st"])
        ts("dve", st[:, 0, :], st[:, 0, :], 1.0 / 64, None, ALU.mult, None, ["st"], ["st"])
        tt("dve", y0, y0, bc(st[:, 0, :], 2, 64), ALU.subtract, yk + ["st"], yk)
        tt("pool", y1, y0, y0, ALU.mult, yk, yk)
        p.op("dve", lambda e: e.tensor_reduce(out=st[:, 1, :], in_=y1, axis=AX.X, op=ALU.add), reads=yk, writes=["st"])
        act(st[:, 1, :], st[:, 1, :], AF.Sqrt, ["st"], ["st"], scale=1.0 / 64, bias=GN_EPS)
        p.op("dve", lambda e: e.reciprocal(out=st[:, 1, :], in_=st[:, 1, :]), reads=["st"], writes=["st"])
        tt("dve", y0, y0, bc(st[:, 1, :], 2, 64), ALU.mult, yk + ["st"], yk)
        tt("dve", y0, y0, bc(lng, 1, 32), ALU.mult, yk + ["lng"], yk)
        ynb = ynb_flat[0].rearrange("p (n v) -> p n v", n=32)
        tt("dve", ynb, y0, bc(lnb, 1, 32), ALU.add, yk + ["lnb"] + tlk, [("tl", 0, 0)])
        mix_nat = mixt.rearrange("p (s c) -> p c s", s=16)
        for hp in range(2):
            fs = slice(hp * 64, hp * 64 + 64)
            for h2 in range(2):
                pi = next_ps()
                pbv = ps[pi][:, :].bitcast(BF16)
                for n8 in range(16):
                    n_ = h2 * 16 + n8
                    p.op("pe", lambda e, n_=n_, n8=n8, pbv=pbv, fs=fs: e.transpose(
                        out=pbv[fs, n8 * 64:(n8 + 1) * 64], in_=ynb[fs, n_, :], identity=i64b[fs, :]),
                        reads=[("tl", 0, 0), "i64b"], writes=[("ps", pi)])
                tks = slice(h2 * 1024, (h2 + 1) * 1024)
                tmpf = MS[fs, tmp_off:tmp_off + 2048].bitcast(F32)
                tt("dve", tmpf, pbv[fs, :], kap[fs, tks], ALU.add, [("ps", pi)] + kapk, [("tmpf", hp), ("tl", 1, 0)])
                tt("pool", mix_nat[fs, h2 * 64:(h2 + 1) * 64, :],
                   tmpf.rearrange("p (c s) -> p c s", s=16), g_bf[fs, tks].rearrange("p (c s) -> p c s", s=16),
                   ALU.mult, [("tmpf", hp), ("tl", 1, 0), "g_bf"], ["mixt", ("tl", 1, 0), ("tl", 1, 1)])
        p.dma("sp", mixT_d[:, 8 + j, :], mixt, "rkmix", reads=["mixt", ("tl", 1, 1)], writes=["mixd"])


def s5_phase(p, nc, env):
    ps, next_ps, hT, MS, din, ident = (env[k] for k in ("ps", "next_ps", "hT", "MS", "din", "ident"))
    mixT_d = env["mixT_d"]
    ar = Arena(MS, 131072)

    w_in_d = env["w_in_d"]
    lamP_d = din("s5_lamP", [8, 64, 3, 2, 8])
    bP_d = din("s5_bP", [8, 64, 2, 8, 16])
    cP_d = din("s5_cP", [8, 64, 2, 8, 16])
    E_d = din("s5_E", [64, 2, 49])
    E2_d = din("s5_E2pi", [64, 2, 49])
    mask_d = din("s5_mask", [128, 2, 2, 256])
    dbc_d = din("s5_d_bc", [128, 1024])
    cidx_d = din("s5_cidx", [64, 2, 128])
    rmask_d = din("s5_rmask", [64, 2, 130])
    wglu_d = din("w_glu_h", [128, 8, 1024])
    bglu_d = din("b_glu_fm", [128, 8])
    ygT_d = nc.dram_tensor("ygT_scratch", [8, 128, S], BF16, kind="Internal").ap()

    E_sb = p.sb([64, 2, 49], F32, "s5E")
    E2_sb = p.sb([64, 2, 49], F32, "s5E2")
    mask_sb = p.sb([128, 2, 2, 256], F32, "s5mask")
    identb = p.sb([128, 128], BF16, "identb")
    bglu = p.sb([128, 8], F32, "bglu")
    cidx = p.sb([64, 2, 128], F32, "s5cidx")
    rmask = p.sb([64, 2, 130], F32, "s5rmask")
    p.dma("sp", cidx[:], cidx_d, "const2", writes=["cidx"])
    p.dma("sp", rmask[:], rmask_d, "const2", writes=["rmask"])
    p.dma("sp", E_sb[:], E_d, "const2", writes=["s5E"])
    p.dma("sp", E2_sb[:], E2_d, "const2", writes=["s5E2"])
    p.dma("sp", mask_sb[:], mask_d, "const2", writes=["s5mask"])
    p.dma("sp", bglu[:], bglu_d, "const2", writes=["bglu"])
    p.op("dve", lambda e: e.tensor_copy(out=identb[:], in_=ident[:]), reads=["ident"], writes=["identb"])

    wic = [ar.alloc(128, [KT, 128], BF16) for _ in range(2)]
    u_tm = ar.alloc(128, [8, 16, 16], BF16)
    uT = ar.alloc(128, [8, 2, 128], BF16)
    TgT = ar.alloc(128, [2, 8, 256], BF16)
    WmT = ar.alloc(128, [2, 2, 2, 8, 64], BF16)
    wmt_off = ar.last
    Wsb_f = ar.alloc(64, [4160], F32)
    Wsb = Wsb_f.rearrange("p (d r g c) -> p d r g c", d=2, r=2, g=8)
    Y_f = ar.alloc(64, [8704], BF16)
    Ytab = Y_f.rearrange("p (d r g n h) -> p d r g n h", d=2, r=2, g=8, n=17)
    Yf = Y_f.rearrange("p (d r g q) -> p d r g q", d=2, r=2, g=8)
    XW_f = ar.alloc(64, [8192], BF16)
    XWt = XW_f.rearrange("p (d r g m h) -> p d r g m h", d=2, r=2, g=8, m=16)
    XWf = XW_f.rearrange("p (d r g q) -> p d r g q", d=2, r=2, g=8)
    HX_f = XW_f[:, 0:4160]
    HX = HX_f.rearrange("p (d r g c) -> p d r g c", d=2, r=2, g=8)
    T1 = ar.alloc(128, [1600], F32)
    T2 = ar.alloc(128, [1600], F32)
    Lre = ar.alloc(64, [2, 8, 49], F32)
    Lim = ar.alloc(64, [2, 8, 49], F32)
    bbar = ar.alloc(64, [2, 2, 8, 16], F32)
    bP = ar.alloc(64, [2, 8, 16], F32)
    cP = ar.alloc(64, [2, 8, 16], F32)
    lamP = ar.alloc(64, [3, 2, 8], F32)
    sm = ar.alloc(64, [12, 2, 8], F32)
    dbc = ar.alloc(128, [128], F32)
    r16 = ar.alloc(64, [2, 8], F32)
    m16 = ar.alloc(64, [2, 8], F32)
    Mdec = ar.alloc(64, [2, 8, 130], F32)
    TWC = ar.alloc(64, [8, 128], F32)
    TWS = ar.alloc(64, [8, 128], F32)

    p.op("pool", lambda e: e.memset(Wsb_f, 0.0), writes=["Wsb"])

    def T1v(parts, shape):
        n = 1
        for d_ in shape:
            n *= d_
        v = T1[0:parts, 0:n]
        names = " ".join("d%d" % i for i in range(len(shape)))
        kw = {"d%d" % i: shape[i] for i in range(1, len(shape))}
        return v.rearrange("p (%s) -> p %s" % (names, names), **kw) if len(shape) > 1 else v

    def T2v(parts, shape):
        n = 1
        for d_ in shape:
            n *= d_
        v = T2[0:parts, 0:n]
        names = " ".join("d%d" % i for i in range(len(shape)))
        kw = {"d%d" % i: shape[i] for i in range(1, len(shape))}
        return v.rearrange("p (%s) -> p %s" % (names, names), **kw) if len(shape) > 1 else v

    def tt(eng, out, a, b, op, r, w):
        p.op(eng, lambda e: e.tensor_tensor(out=out, in0=a, in1=b, op=op), reads=r, writes=w)

    def ts(eng, out, a, s1, s2, op0, op1, r, w):
        if op1 is None:
            p.op(eng, lambda e: e.tensor_scalar(out=out, in0=a, scalar1=s1, scalar2=None, op0=op0), reads=r, writes=w)
        else:
            p.op(eng, lambda e: e.tensor_scalar(out=out, in0=a, scalar1=s1, scalar2=s2, op0=op0, op1=op1),
                 reads=r, writes=w)

    def act(out, in_, func, r, w, **kw):
        p.op("act", lambda e: e.activation(out=out, in_=in_, func=func, **kw), reads=r, writes=w)

    for gb in range(8):
        p.dma("sp", lamP, lamP_d[gb], "s5p0", writes=["lamP"])
        p.dma("sp", bP, bP_d[gb], "s5p1", writes=["bP"])
        p.dma("sp", cP, cP_d[gb], "s5p2", writes=["cP"])
        p.dma("sp", dbc, dbc_d[:, gb * 128:(gb + 1) * 128], "s5p3", writes=["dbc"])
        b = gb % 2
        p.dma("pool", wic[b], w_in_d[:, gb], "wic%d" % b, writes=[("wic", b)], max_dma_last_dim=4096)
        for s4 in range(4):
            pi = next_ps()
            for i in range(4):
                s_ = s4 * 4 + i
                for kt in range(KT):
                    p.op("pe", lambda e, kt=kt, pi=pi, i=i, s_=s_, b=b: e.matmul(
                        ps[pi][:, i * 128:(i + 1) * 128], lhsT=hT[:, kt, s_ * 128:(s_ + 1) * 128],
                        rhs=wic[b][:, kt, :], start=(kt == 0), stop=(kt == KT - 1)),
                        reads=[("wic", b)], writes=[("ps", pi)])
            act(u_tm[:, :, s4 * 4:(s4 + 1) * 4, :].rearrange("p g s h -> p s g h"),
                ps[pi][:, :].rearrange("p (s g h) -> p s g h", s=4, g=8), AF.Copy,
                [("ps", pi)], ["u_tm"])
        for half in range(2):
            pi = next_ps()
            pb = ps[pi][:, :].bitcast(BF16)
            for g in range(8):
                p.op("pe", lambda e, g=g, half=half, pb=pb: e.transpose(
                    out=pb[:, g * 128:(g + 1) * 128],
                    in_=u_tm[:, g, half * 8:(half + 1) * 8, :].rearrange("p s h -> p (s h)"),
                    identity=identb[:]), reads=["u_tm", "identb"], writes=[("ps", pi)])
            p.op("dve", lambda e, half=half, pb=pb: e.tensor_copy(
                out=uT[:, :, half, :], in_=pb.rearrange("p (g c) -> p g c", g=8)),
                reads=[("ps", pi)], writes=["uT"])
        lr, li, lst = lamP[:, 0], lamP[:, 1], lamP[:, 2]
        step, lrs, th = sm[:, 0], sm[:, 1], sm[:, 2]
        act(step, lst, AF.Exp, ["lamP"], ["sm0"])
        tt("dve", lrs, lr, step, ALU.mult, ["lamP", "sm0"], ["sm1"])
        tt("dve", th, li, step, ALU.mult, ["lamP", "sm0"], ["sm2"])
        ang = T1v(64, [2, 8, 49])
        lgm = T2v(64, [2, 8, 49])
        qi = T1[0:64, 784:784 + 784].bitcast(I32).rearrange("p (a b c) -> p a b c", a=2, b=8)
        qf = T2[0:64, 784:784 + 784].rearrange("p (a b c) -> p a b c", a=2, b=8)
        tt("dve", ang, bc(th, 3, 49), bc(E2_sb[:], 2, 8), ALU.mult, ["sm2", "s5E2"], ["T1"])
        tt("dve", lgm, bc(lrs, 3, 49), bc(E_sb[:], 2, 8), ALU.mult, ["sm1", "s5E"], ["T2"])
        p.op("dve", lambda e: e.tensor_copy(out=qi, in_=ang), reads=["T1"], writes=["T1"])
        p.op("dve", lambda e: e.tensor_copy(out=qf, in_=qi), reads=["T1"], writes=["T2"])
        tt("dve", ang, ang, qf, ALU.subtract, ["T1", "T2"], ["T1"])
        ts("dve", qf, ang, 0.5, None, ALU.is_gt, None, ["T1"], ["T2"])
        tt("dve", ang, ang, qf, ALU.subtract, ["T1", "T2"], ["T1"])
        ts("dve", qf, ang, -0.5, None, ALU.is_lt, None, ["T1"], ["T2"])
        tt("dve", ang, ang, qf, ALU.add, ["T1", "T2"], ["T1"])
        for d in range(2):
            eA = 16 if d == 0 else 0
            p.op("dve", lambda e, d=d, eA=eA: e.tensor_copy(out=r16[:, d], in_=ang[:, d, :, eA]),
                 reads=["T1"], writes=["r16"])
        sinv = qf
        act(sinv, ang, AF.Sin, ["T1"], ["T2"], scale=TWO_PI)
        act(ang, ang, AF.Abs, ["T1"], ["T1"])
        cosv = T1[0:64, 784:784 + 784].rearrange("p (a b c) -> p a b c", a=2, b=8)
        act(cosv, ang, AF.Sin, ["T1"], ["T1"], scale=-TWO_PI, bias=1.5707963267948966)
        act(lgm, lgm, AF.Exp, ["T2"], ["T2"])
        for d in range(2):
            eA = 16 if d == 0 else 0
            p.op("dve", lambda e, d=d, eA=eA: e.tensor_copy(out=m16[:, d], in_=lgm[:, d, :, eA]),
                 reads=["T2"], writes=["m16"])
        tt("dve", Lre, lgm, cosv, ALU.mult, ["T2", "T1"], ["Lre"])
        tt("pool", Lim, lgm, sinv, ALU.mult, ["T2", "T2"], ["Lim"])
        nr, ni, den, cre, cim, t_a, t_b = (sm[:, i] for i in range(3, 10))
        for d in range(2):
            e1 = 1 if d == 0 else 15
            ts("dve", nr[:, d], Lre[:, d, :, e1], -1.0, None, ALU.add, None, ["Lre"], ["sm3"])
            p.op("dve", lambda e, d=d, e1=e1: e.tensor_copy(out=ni[:, d], in_=Lim[:, d, :, e1]), reads=["Lim"], writes=["sm4"])
        tt("dve", den, lr, lr, ALU.mult, ["lamP"], ["sm5"])
        tt("dve", t_a, li, li, ALU.mult, ["lamP"], ["sm8"])
        tt("dve", den, den, t_a, ALU.add, ["sm5", "sm8"], ["sm5"])
        p.op("dve", lambda e: e.reciprocal(out=den, in_=den), reads=["sm5"], writes=["sm5"])
        tt("dve", cre, nr, lr, ALU.mult, ["sm3", "lamP"], ["sm6"])
        tt("dve", t_a, ni, li, ALU.mult, ["sm4", "lamP"], ["sm8"])
        tt("dve", cre, cre, t_a, ALU.add, ["sm6", "sm8"], ["sm6"])
        tt("dve", cre, cre, den, ALU.mult, ["sm6", "sm5"], ["sm6"])
        tt("dve", cim, ni, lr, ALU.mult, ["sm4", "lamP"], ["sm7"])
        tt("dve", t_a, nr, li, ALU.mult, ["sm3", "lamP"], ["sm8"])
        tt("dve", cim, cim, t_a, ALU.subtract, ["sm7", "sm8"], ["sm7"])
        tt("dve", cim, cim, den, ALU.mult, ["sm7", "sm5"], ["sm7"])
        for d in range(2):
            x1 = T1v(64, [8, 16])
            x2 = T2v(64, [8, 16])
            crb, cib = bc(cre[:, d], 2, 16), bc(cim[:, d], 2, 16)
            tt("dve", x1, crb, bP[:, 0], ALU.mult, ["sm6", "bP"], ["T1"])
            tt("dve", x2, cib, bP[:, 1], ALU.mult, ["sm7", "bP"], ["T2"])
            tt("dve", bbar[:, d, 0], x1, x2, ALU.subtract, ["T1", "T2"], ["bbar"])
            tt("dve", x1, crb, bP[:, 1], ALU.mult, ["sm6", "bP"], ["T1"])
            tt("dve", x2, cib, bP[:, 0], ALU.mult, ["sm7", "bP"], ["T2"])
            tt("dve", bbar[:, d, 1], x1, x2, ALU.add, ["T1", "T2"], ["bbar"])
        for d in range(2):
            for gh in range(2):
                gs = slice(gh * 4, gh * 4 + 4)
                x1 = T1v(64, [4, 17, 16])
                x2 = T2v(64, [4, 17, 16])
                Lr, Li = bc(Lre[:, d, gs, 0:17], 3, 16), bc(Lim[:, d, gs, 0:17], 3, 16)
                Cr, Ci = bc(cP[:, 0, gs], 2, 17), bc(cP[:, 1, gs], 2, 17)
                tt("dve", x1, Cr, Lr, ALU.mult, ["cP", "Lre"], ["T1"])
                tt("pool", x2, Ci, Li, ALU.mult, ["cP", "Lim"], ["T2"])
                tt("dve", Ytab[:, d, 0, gs], x1, x2, ALU.subtract, ["T1", "T2"], ["Ytab"])
                tt("dve", x1, Cr, Li, ALU.mult, ["cP", "Lim"], ["T1"])
                tt("pool", x2, Ci, Lr, ALU.mult, ["cP", "Lre"], ["T2"])
                ts("dve", x1, x1, -1.0, None, ALU.mult, None, ["T1"], ["T1"])
                tt("dve", Ytab[:, d, 1, gs], x1, x2, ALU.subtract, ["T1", "T2"], ["Ytab"])

        def gen_xw(e0):
            for d in range(2):
                for gh in range(2):
                    gs = slice(gh * 4, gh * 4 + 4)
                    x1 = T1v(64, [4, 16, 16])
                    x2 = T2v(64, [4, 16, 16])
                    Lr, Li = bc(Lre[:, d, gs, e0:e0 + 16], 3, 16), bc(Lim[:, d, gs, e0:e0 + 16], 3, 16)
                    Br, Bi = bc(bbar[:, d, 0, gs], 2, 16), bc(bbar[:, d, 1, gs], 2, 16)
                    tt("dve", x1, Lr, Br, ALU.mult, ["Lre", "bbar"], ["T1"])
                    tt("pool", x2, Li, Bi, ALU.mult, ["Lim", "bbar"], ["T2"])
                    tt("dve", XWt[:, d, 0, gs], x1, x2, ALU.subtract, ["T1", "T2"], ["XWt"])
                    tt("dve", x1, Lr, Bi, ALU.mult, ["Lre", "bbar"], ["T1"])
                    tt("pool", x2, Li, Br, ALU.mult, ["Lim", "bbar"], ["T2"])
                    tt("dve", XWt[:, d, 1, gs], x1, x2, ALU.add, ["T1", "T2"], ["XWt"])

        gen_xw(17)
        for g in range(8):
            for half in range(2):
                pi = next_ps()
                for d in range(2):
                    n0 = 0 if d == 0 else 1
                    for ri in range(2):
                        p.op("pe", lambda e, g=g, half=half, d=d, ri=ri, n0=n0, pi=pi: e.matmul(
                            ps[pi][:, d * 256:(d + 1) * 256],
                            lhsT=XWf[:, d, ri, g, half * 128:(half + 1) * 128],
                            rhs=Yf[:, d, ri, g, n0 * 16:n0 * 16 + 256],
                            start=(ri == 0), stop=(ri == 1)),
                            reads=["XWt", "Ytab"], writes=[("ps", pi)])
                m1 = T1v(128, [512])
                tt("dve", m1, ps[pi][:, :], mask_sb[:, half].rearrange("p a b -> p (a b)"), ALU.mult,
                   [("ps", pi), "s5mask"], ["T1"])
                tt("pool", TgT[:, half, g, :], m1[:, 0:256], m1[:, 256:512], ALU.add, ["T1"], ["TgT"])
        gen_xw(33)
        for d in range(2):
            for hh in range(2):
                ri = hh
                pi = next_ps()
                pb = ps[pi][:, :].bitcast(BF16)
                for half in range(2):
                    for g in range(8):
                        col = (half * 8 + g) * 64
                        p.op("pe", lambda e, d=d, ri=ri, half=half, g=g, col=col, pb=pb: e.transpose(
                            out=pb[:, col:col + 64], in_=XWf[:, d, ri, g, half * 128:(half + 1) * 128],
                            identity=identb[0:64, 0:64]), reads=["XWt", "identb"], writes=[("ps", pi)])
                p.op("act", lambda e, d=d, ri=ri, pb=pb: e.activation(
                    out=WmT[:, :, d, ri, :, :], in_=pb.rearrange("p (h g q) -> p h g q", h=2, g=8), func=AF.Copy),
                    reads=[("ps", pi)], writes=["WmT"])
        for g in range(8):
            pi = next_ps()
            for d in range(2):
                for ri in range(2):
                    c0 = (d * 2 + ri) * 128
                    for half in range(2):
                        p.op("pe", lambda e, g=g, d=d, ri=ri, half=half, c0=c0, pi=pi: e.matmul(
                            ps[pi][0:64, c0:c0 + 128],
                            lhsT=WmT[:, half, d, ri, g, :], rhs=uT[:, g, half, :],
                            start=(half == 0), stop=(half == 1)),
                            reads=["WmT", "uT"], writes=[("ps", pi)])
            p.op("act", lambda e, g=g, pi=pi: e.activation(
                out=Wsb[:, :, :, g, 1:129], in_=ps[pi][0:64, :].rearrange("p (d r c) -> p d r c", d=2, r=2),
                func=AF.Copy), reads=[("ps", pi)], writes=["Wsb"])
        for d in range(2):
            tt("dve", Mdec[:, d], bc(m16[:, d], 2, 130), bc(rmask[:, d], 1, 8), ALU.mult, ["m16", "rmask"], ["Mdec"])
        p.op("pool", lambda e: e.memset(HX[:, :, :, :, 0:1].rearrange("p d r g c -> p (d r g c)"), 0.0),
             reads=[], writes=["XWt"])
        p.op("pool", lambda e: e.memset(HX[:, :, :, :, 129:130].rearrange("p d r g c -> p (d r g c)"), 0.0),
             reads=[], writes=["XWt"])
        for d in range(2):
            a1 = T1[0:64, 0:1024].rearrange("p (g c) -> p g c", g=8)
            a2 = T2[0:64, 0:1024].rearrange("p (g c) -> p g c", g=8)
            qi2 = T2[0:64, 0:1024].bitcast(I32).rearrange("p (g c) -> p g c", g=8)
            tt("dve", a1, bc(r16[:, d], 2, 128), bc(cidx[:, d], 1, 8), ALU.mult, ["r16", "cidx"], ["T1"])
            p.op("dve", lambda e: e.tensor_copy(out=qi2, in_=a1), reads=["T1"], writes=["T2"])
            p.op("dve", lambda e: e.tensor_copy(out=TWC, in_=qi2), reads=["T2"], writes=["TWC"])
            tt("dve", a1, a1, TWC, ALU.subtract, ["T1", "TWC"], ["T1"])
            ts("dve", TWC, a1, 0.5, None, ALU.is_gt, None, ["T1"], ["TWC"])
            tt("dve", a1, a1, TWC, ALU.subtract, ["T1", "TWC"], ["T1"])
            ts("dve", TWC, a1, -0.5, None, ALU.is_lt, None, ["T1"], ["TWC"])
            tt("dve", a1, a1, TWC, ALU.add, ["T1", "TWC"], ["T1"])
            act(TWS, a1, AF.Sin, ["T1"], ["TWS"], scale=TWO_PI)
            act(a1, a1, AF.Abs, ["T1"], ["T1"])
            act(TWC, a1, AF.Sin, ["T1"], ["TWC"], scale=-TWO_PI, bias=1.5707963267948966)
            Wre = Wsb[:, d, 0, :, 1:129]
            Wim = Wsb[:, d, 1, :, 1:129]
            tt("dve", a1, Wre, TWC, ALU.mult, ["Wsb", "TWC"], ["T1"])
            tt("pool", a2, Wim, TWS, ALU.mult, ["Wsb", "TWS"], ["T2"])
            tt("dve", a1, a1, a2, ALU.add, ["T1", "T2"], ["T1"])
            tt("pool", a2, Wre, TWS, ALU.mult, ["Wsb", "TWS"], ["T2"])
            cp_ = lambda eng, o, i_, r, w: p.op(eng, lambda e: e.tensor_copy(out=o, in_=i_), reads=r, writes=w)
            cp_("dve", Wre, a1, ["T1", "T2"], ["Wsb"])
            tt("dve", a1, Wim, TWC, ALU.mult, ["Wsb", "TWC"], ["T1"])
            tt("dve", Wim, a1, a2, ALU.subtract, ["T1", "T2"], ["Wsb"])
            for ri in range(2):
                seg = Wsb[:, d, ri].rearrange("p g c -> p (g c)")
                dec = Mdec[:, d].rearrange("p g c -> p (g c)")
                if d == 1:
                    seg = seg[:, ::-1]
                    dec = dec[:, ::-1]
                p.op("dve", lambda e, seg=seg, dec=dec: e.tensor_tensor_scan(
                    out=seg, data0=dec, data1=seg, initial=0.0, op0=ALU.mult, op1=ALU.add),
                    reads=["Wsb", "Mdec"], writes=["Wsb"])
            tt("dve", a1, Wre, TWC, ALU.mult, ["Wsb", "TWC"], ["T1"])
            tt("pool", a2, Wim, TWS, ALU.mult, ["Wsb", "TWS"], ["T2"])
            tt("dve", HX[:, d, 0, :, 1:129], a1, a2, ALU.subtract, ["T1", "T2"], ["XWt"])
            tt("dve", a1, Wim, TWC, ALU.mult, ["Wsb", "TWC"], ["T1"])
            tt("pool", a2, Wre, TWS, ALU.mult, ["Wsb", "TWS"], ["T2"])
            tt("dve", HX[:, d, 1, :, 1:129], a1, a2, ALU.add, ["T1", "T2"], ["XWt"])
        ytm = T2[:, 0:1024].bitcast(BF16).rearrange("p (t c) -> p t c", t=16)
        WB = MS[:, wmt_off:wmt_off + 4096].bitcast(F32)

        def gelu_chain(g2, R, rk):
            pi = next_ps()
            for gg in range(2):
                g = g2 * 2 + gg
                o = ps[pi][:, gg * 256:(gg + 1) * 256]
                mms = [(uT[:, g, 0, :], TgT[:, 0, g, :]), (uT[:, g, 1, :], TgT[:, 1, g, :])]
                for ri in range(2):
                    mms.append((HX[:, 0, ri, g, 0:128], Yf[:, 0, ri, g, 16:272]))
                    mms.append((HX[:, 1, ri, g, 2:130], Yf[:, 1, ri, g, 0:256]))
                for i, (l_, r_) in enumerate(mms):
                    p.op("pe", lambda e, o=o, l_=l_, r_=r_, i=i, n=len(mms): e.matmul(
                        o, lhsT=l_, rhs=r_, start=(i == 0), stop=(i == n - 1)),
                        reads=["uT", "TgT", "XWt", "Ytab"], writes=[("ps", pi)])
            yield
            yv = R[:, 0:512].rearrange("p (g t h) -> p g t h", g=2, t=16)
            z1 = R[:, 512:1024].rearrange("p (g t h) -> p g t h", g=2, t=16)
            z2 = R[:, 1024:1536].rearrange("p (g t h) -> p g t h", g=2, t=16)
            for gg in range(2):
                g = g2 * 2 + gg
                uview = u_tm[:, g, :, :]
                tt("dve", z1[:, gg], uview, bc(dbc[:, g * 16:(g + 1) * 16], 1, 16), ALU.mult,
                   ["u_tm", "dbc"], [rk])
                tt("dve", yv[:, gg], z1[:, gg],
                   ps[pi][:, gg * 256:(gg + 1) * 256].rearrange("p (t h) -> p t h", t=16), ALU.add,
                   [rk, ("ps", pi)], [rk])
            yield
            yf = R[:, 0:512]
            z1f = R[:, 512:1024]
            z2f = R[:, 1024:1536]
            tt("pool", z1f, yf, yf, ALU.mult, [rk], [rk])
            yield
            ts("dve", z1f, z1f, 0.044715, 1.0, ALU.mult, ALU.add, [rk], [rk])
            yield
            tt("pool", z1f, z1f, yf, ALU.mult, [rk], [rk])
            yield
            act(z2f, z1f, AF.Sigmoid, [rk], [rk], scale=1.5957691216057308)
            yield
            for gg in range(2):
                g = g2 * 2 + gg
                tt("dve", ytm[:, :, g * 16:(g + 1) * 16], yv[:, gg], z2[:, gg], ALU.mult, [rk], ["T2"])
            yield

        run_gens([gelu_chain(0, T1, "T1"), gelu_chain(1, WB, "WmT")])
        run_gens([gelu_chain(2, T1, "T1"), gelu_chain(3, WB, "WmT")])
        ygT = T1[:, 0:1024].bitcast(BF16)
        for h2 in range(2):
            pi = next_ps()
            pb = ps[pi][:, :].bitcast(BF16)
            for t8 in range(8):
                t_ = h2 * 8 + t8
                p.op("pe", lambda e, t_=t_, t8=t8, pb=pb: e.transpose(
                    out=pb[:, t8 * 128:(t8 + 1) * 128], in_=ytm[:, t_, :], identity=identb[:]),
                    reads=["T2", "identb"], writes=[("ps", pi)])
            p.op("act", lambda e, h2=h2, pb=pb: e.activation(out=ygT[:, h2 * 1024:(h2 + 1) * 1024], in_=pb, func=AF.Copy),
                 reads=[("ps", pi), "T1", "T1", "T1"], writes=["T1"])
        p.dma("sp", ygT_d[gb], ygT, "ygw", reads=["T1"], writes=[("ygd", gb)])

    p.barrier()
    ar.reset()
    ygA = ar.alloc(128, [8, S], BF16)
    wg = [ar.alloc(128, [8, 128], BF16) for _ in range(2)]
    sgb = [ar.alloc(128, [512], F32) for _ in range(2)]
    mxb = [ar.alloc(128, [512], BF16) for _ in range(2)]
    p.dma("sp", ygA, ygT_d.rearrange("k p t -> p k t"), "ygld", writes=["ygA"])
    for nt in range(8):
        b = nt % 2
        p.dma("pool", wg[b], wglu_d[:, :, nt * 128:(nt + 1) * 128], "wg%d" % b, writes=[("wg", b)])
        for tb in range(4):
            pi = next_ps()
            for kt in range(8):
                p.op("pe", lambda e, kt=kt, pi=pi, tb=tb, b=b: e.matmul(
                    ps[pi][:, :], lhsT=wg[b][:, kt, :], rhs=ygA[:, kt, tb * 512:(tb + 1) * 512],
                    start=(kt == 0), stop=(kt == 7)), reads=[("wg", b), "ygA"], writes=[("ps", pi)])
            r = tb % 2
            act(sgb[r], ps[pi][:, :], AF.Sigmoid, [("ps", pi), "bglu"], [("sgb", r)], bias=bglu[:, nt:nt + 1])
            tt("dve", mxb[r], sgb[r], ygA[:, nt, tb * 512:(tb + 1) * 512], ALU.mult, [("sgb", r), "ygA"], [("mxb", r)])
            p.dma("sp", mixT_d[:, nt, tb * 512:(tb + 1) * 512], mxb[r], "mxw%d" % r, reads=[("mxb", r)],
                  writes=["mixd"])


def prep_inputs(inputs):
    f = lambda a: np.ascontiguousarray(np.asarray(a, dtype=np.float32))
    sh = {}
    ada_w = f(inputs["ada_w"])[0]
    sh["ada_w"] = f(ada_w.reshape(KT, 128, 12, 1024).transpose(1, 2, 0, 3))
    ada_b = f(inputs["ada_b"])[0]
    sh["ada_b_fm"] = f(ada_b.reshape(96, 128).T)
    gb = np.concatenate([ada_b[2 * D:3 * D], ada_b[5 * D:6 * D]])
    sh["ada_b_bc"] = f(np.broadcast_to(gb[None, :], (128, 2 * D)))
    sh["g1_fm"] = f(f(inputs["norm1_gain"])[0].reshape(KT, 128).T)
    sh["g2_fm"] = f(f(inputs["norm2_gain"])[0].reshape(KT, 128).T)
    sh["ident"] = np.eye(128, dtype=np.float32)
    sh["w_out"] = f(f(inputs["w_out"])[0].reshape(KT, 128, 4, 512).transpose(1, 2, 0, 3))
    sh["ffn_w1"] = f(f(inputs["ffn_w1"])[0].reshape(KT, 128, FFN // 256, 256).transpose(1, 2, 0, 3))
    sh["ffn_w2"] = f(f(inputs["ffn_w2"])[0].reshape(8, 8, 128, 4, 512).transpose(2, 3, 0, 1, 4))
    sh["fgain_bc"] = f(np.broadcast_to(f(inputs["final_gain"])[None, :], (128, D)))
    w_in_pad = np.zeros((D, 36 * 128), np.float32)
    w_in_pad[:, :PROJ] = f(inputs["w_in"])[0]
    sh["w_in_h"] = f(w_in_pad.reshape(KT, 128, 36, 128).transpose(1, 2, 0, 3))
    lre, lim = f(inputs["s5_lambda_re"])[0], f(inputs["s5_lambda_im"])[0]
    lst = f(inputs["s5_log_step"])[0]
    lam = np.stack([lre, lim, np.broadcast_to(lst[:, :, None], lre.shape)], 0)
    sh["s5_lamP"] = f(lam.reshape(3, 2, 8, 8, 64).transpose(2, 4, 0, 1, 3))
    bre, bim = f(inputs["s5_b_re"])[0], f(inputs["s5_b_im"])[0]
    bb = np.stack([bre, bim], 0)
    sh["s5_bP"] = f(bb.reshape(2, 8, 8, 64, 16).transpose(1, 3, 0, 2, 4))
    cre_, cim_ = f(inputs["s5_c_re"])[0], f(inputs["s5_c_im"])[0]
    cc = np.stack([cre_, cim_], 0)
    sh["s5_cP"] = f(cc.reshape(2, 8, 8, 16, 64).transpose(1, 4, 0, 2, 3))
    n17 = np.arange(17, dtype=np.float32)
    m16 = np.arange(16, dtype=np.float32)
    E = np.stack([np.concatenate([n17, -m16, 15 - m16]), np.concatenate([16 - n17, m16 - 15, m16])], 0)
    sh["s5_E"] = f(np.broadcast_to(E[None], (64, 2, 49)))
    sh["s5_E2pi"] = f(np.broadcast_to((E / (2 * np.pi))[None], (64, 2, 49)))
    s8 = np.arange(128) // 16
    tt_ = np.arange(256) // 16
    mask = np.zeros((128, 2, 2, 256), np.float32)
    for half in range(2):
        sv = half * 8 + s8
        mask[:, half, 0, :] = (tt_[None, :] >= sv[:, None])
        mask[:, half, 1, :] = (tt_[None, :] <= sv[:, None])
    sh["s5_mask"] = mask
    ci = np.arange(128, dtype=np.float32)
    sh["s5_cidx"] = f(np.broadcast_to(np.stack([ci, 127 - ci], 0)[None], (64, 2, 128)))
    rm = np.ones((2, 130), np.float32)
    rm[0, 0] = 0.0
    rm[1, 129] = 0.0
    sh["s5_rmask"] = f(np.broadcast_to(rm[None], (64, 2, 130)))
    sh["s5_d_bc"] = f(np.broadcast_to(f(inputs["s5_d"])[0][None, :], (128, 1024)))
    mup, mun = f(inputs["rk_shift_prev"])[0], f(inputs["rk_shift_next"])[0]
    prm = np.zeros((128, 112), np.float32)
    for t_ in range(28):
        n_ = min(128, 3488 - t_ * 128)
        prm[:n_, t_] = mup[t_ * 128:t_ * 128 + n_]
        prm[:n_, 28 + t_] = mun[t_ * 128:t_ * 128 + n_]
    w0_, a0_ = f(inputs["rk_w0"])[0], f(inputs["rk_a0"])[0]
    for d_ in range(2):
        prm[:, 56 + d_ * 8:56 + d_ * 8 + 8] = w0_[d_].reshape(8, 128).T
        prm[:, 72 + d_ * 8:72 + d_ * 8 + 8] = a0_[d_].reshape(8, 128).T
    prm[:, 88:96] = f(inputs["rk_k_k"])[0].reshape(8, 128).T
    prm[:, 96:104] = f(inputs["rk_k_a"])[0].reshape(8, 128).T
    prm[:, 104:112] = f(inputs["rk_r_k"])[0].reshape(8, 128).T
    sh["rk_prm"] = prm
    sh["rk_wup"] = f(f(inputs["rk_w_up"])[0].reshape(128, 1024))
    sh["rk_aup"] = f(f(inputs["rk_a_up"])[0].reshape(128, 1024))
    gup = f(inputs["rk_g_up"])[0]
    sh["rk_gup1"] = f(gup[0:128])
    sh["rk_gup2"] = f(gup[128:160])
    sh["rk_lng"] = f(np.repeat(f(inputs["rk_ln_gain"])[0].reshape(8, 2, 64), 64, axis=1))
    sh["rk_lnb"] = f(np.repeat(f(inputs["rk_ln_bias"])[0].reshape(8, 2, 64), 64, axis=1))
    sm_ = np.ones((128, 512), np.float32)
    sm_[:, ::64] = 0.0
    sh["rk_smask"] = sm_
    ii = np.arange(64)
    m5 = np.zeros((128, 2, 320), np.float32)
    for d_ in range(2):
        if d_ == 0:
            strict = (ii[:, None] < ii[None, :]).astype(np.float32)
            incl = (ii[:, None] <= ii[None, :]).astype(np.float32)
        else:
            strict = (ii[:, None] > ii[None, :]).astype(np.float32)
            incl = (ii[:, None] >= ii[None, :]).astype(np.float32)
        row = np.concatenate([-strict, incl, strict, incl, -strict.T], axis=1)
        m5[0:64, d_] = row
        m5[64:128, d_] = row
    sh["rk_mask5"] = m5
    sh["rk_i64"] = f(np.concatenate([np.eye(64), np.eye(64)], 0))
    bo = np.zeros((128, 128), np.float32)
    bo[0:64, 0:64] = 1.0
    bo[64:128, 64:128] = 1.0
    sh["rk_bones"] = bo
    sh["w_glu_h"] = f(f(inputs["s5_w_glu"])[0].reshape(8, 128, 1024).transpose(1, 0, 2))
    sh["b_glu_fm"] = f(f(inputs["s5_b_glu"])[0].reshape(8, 128).T)
    per = []
    x = f(inputs["x"])
    c = f(inputs["c"])
    for b in range(8):
        per.append({"x": x[b], "c_col": f(c[b].reshape(KT, 128).T)})
    return sh, per


_NC_CACHE = {}


def kernel(**inputs):
    sh, per = prep_inputs(inputs)
    if "main" not in _NC_CACHE:
        _NC_CACHE["main"] = build_program()
    nc = _NC_CACHE["main"]
    in_maps = [dict(sh, **per[b]) for b in range(8)]
    res = run_bass_kernel_spmd(nc, in_maps, core_ids=list(range(8)))
    out = np.stack([np.asarray(r["out"], dtype=np.float32) for r in res.results], axis=0)
    return out
```

```python
import contextlib
import numpy as np
import ml_dtypes
import concourse.bass as bass
import concourse.mybir as mybir
from concourse.bass_utils import run_bass_kernel_spmd

F32 = mybir.dt.float32
BF16 = mybir.dt.bfloat16
ALU = mybir.AluOpType
AF = mybir.ActivationFunctionType
AX = mybir.AxisListType

D = 2048
S = 2048
NT = 16
KT = 16
PROJ = 4512
FFN = 8192
EPS = 1e-6

SAME_SYNC = ("act", "dve", "pool")


class Prog:
    ENG = ("pe", "act", "dve", "pool", "sp")

    def __init__(self, nc, stack):
        self.nc = nc
        self.stack = stack
        self.q = {e: [] for e in self.ENG}
        self.cnt = {e: 0 for e in self.ENG}
        self.sem = {e: stack.enter_context(nc.semaphore("s_" + e)) for e in self.ENG}
        self.seen = {e: {} for e in self.ENG}
        self.lastw = {}
        self.readers = {}
        self.dsem = {}
        self.dcnt = {}
        self.nsb = 0

    def sb(self, shape, dtype, name=None):
        self.nsb += 1
        return self.stack.enter_context(
            self.nc.sbuf_tensor("sb_" + (name or ("t%d" % self.nsb)), list(shape), dtype))

    GROUP = ("const", "dbg", "const2")

    def _tokval(self, tok):
        if tok[0] == "e":
            return tok[1], self.sem[tok[1]], tok[2]
        if tok[1] in self.GROUP:
            return "d:" + tok[1], self.dsem[tok[1]], 1 << 40
        return "d:" + tok[1], self.dsem[tok[1]], tok[2]

    def _deps(self, eng, reads, writes):
        need = {}

        def add(tok, war=False):
            name, sem, val = self._tokval(tok)
            if name == eng:
                if eng not in SAME_SYNC:
                    return
            if need.get(name, (None, 0))[1] < val:
                need[name] = (sem, val)

        for k in reads:
            t = self.lastw.get(k)
            if t:
                add(t)
        for k in writes:
            t = self.lastw.get(k)
            if t:
                add(t)
            for t in self.readers.get(k, {}).values():
                add(t, war=True)
        waits = []
        for name, (sem, val) in need.items():
            if self.seen[eng].get(name, 0) < val:
                self.seen[eng][name] = val
                waits.append((sem, val))
        return waits

    def _record(self, tok, reads, writes):
        name = self._tokval(tok)[0]
        for k in writes:
            self.lastw[k] = tok
            self.readers[k] = {}
        for k in reads:
            self.readers.setdefault(k, {})[name] = tok

    def op(self, eng, fn, reads=(), writes=()):
        waits = self._deps(eng, reads, writes)
        self.cnt[eng] += 1
        tok = ("e", eng, self.cnt[eng])
        self.q[eng].append((waits, fn, self.sem[eng], 1))
        self._record(tok, reads, writes)
        return tok

    def dma(self, eng, out, in_, slot, reads=(), writes=(), **kw):
        if slot not in self.dsem:
            self.dsem[slot] = self.stack.enter_context(self.nc.semaphore("d_" + slot))
            self.dcnt[slot] = 0
        waits = self._deps(eng, reads, writes)
        self.dcnt[slot] += 16
        tok = ("d", slot, self.dcnt[slot])
        self.q[eng].append((waits, lambda e: e.dma_start(out=out, in_=in_, **kw),
                            self.dsem[slot], 16))
        self._record(tok, reads, writes)
        return tok

    def barrier(self):
        for e in self.ENG:
            waits = []
            for e2 in self.ENG:
                if e2 != e and self.cnt[e2] > self.seen[e].get(e2, 0):
                    self.seen[e][e2] = self.cnt[e2]
                    waits.append((self.sem[e2], self.cnt[e2]))
            for slot, c in self.dcnt.items():
                name = "d:" + slot
                if c > self.seen[e].get(name, 0):
                    self.seen[e][name] = c
                    waits.append((self.dsem[slot], c))
            if waits:
                self.q[e].append((waits, None, None, 0))
        self.lastw = {}
        self.readers = {}

    def emit(self):
        nc = self.nc
        with nc.Block() as block:
            for name, deco in (("pe", block.tensor), ("act", block.scalar),
                               ("dve", block.vector), ("pool", block.gpsimd),
                               ("sp", block.sync)):
                def body(e, name=name):
                    for waits, fn, sem, inc in self.q[name]:
                        for s_, v_ in waits:
                            if v_ >= (1 << 40):
                                v_ = [self.dcnt[k] for k in self.GROUP if self.dsem.get(k) is s_][0]
                            e.wait_ge(s_, v_)
                        if fn is not None:
                            fn(e).then_inc(sem, inc)
                deco(body)


def build_program(debug=None):
    nc = bass.Bass("TRN2", target_bir_lowering=False)
    dbg = {}
    with contextlib.ExitStack() as stack:
        p = Prog(nc, stack)

        def din(name, shape, dt=F32):
            return nc.dram_tensor(name, list(shape), dt, kind="ExternalInput").ap()

        x_d = din("x", [S, D])
        c_d = din("c_col", [128, KT])
        adaw_d = din("ada_w", [128, 12, KT, 1024])
        adab_fm_d = din("ada_b_fm", [128, 96])
        adab_bc_d = din("ada_b_bc", [128, 2 * D])
        g1_d = din("g1_fm", [128, KT])
        g2_d = din("g2_fm", [128, KT])
        ident_d = din("ident", [128, 128])
        out_d = nc.dram_tensor("out", [S, D], F32, kind="ExternalOutput").ap()
        x_t = x_d.rearrange("(c s) d -> c s d", s=NT)
        out_t = out_d.rearrange("(c s) d -> c s d", s=NT)

        ps = [stack.enter_context(nc.psum_tensor("ps%d" % i, [128, 512], F32)) for i in range(8)]
        psrr = [0]

        def next_ps():
            i = psrr[0] % 8
            psrr[0] += 1
            return i

        BIG1 = p.sb([128, 32768], BF16, "big1")
        MS = p.sb([128, 65536], BF16, "ms")
        BIG2 = MS[:, 0:32768]
        ident = p.sb([128, 128], F32, "ident")
        p.dma("sp", ident[:], ident_d, "const", writes=["ident"])
        ones = p.sb([128, 128], F32, "ones")
        p.op("dve", lambda e: e.memset(ones[:], 1.0), writes=["ones"])

        c_col = p.sb([128, KT], F32, "c_col")
        p.dma("sp", c_col[:], c_d, "const", writes=["c_col"])
        c_act = p.sb([128, KT], F32, "c_act")
        p.op("act", lambda e: e.activation(out=c_act[:], in_=c_col[:], func=AF.Silu),
             reads=["c_col"], writes=["c_act"])
        c_actb = p.sb([128, KT], BF16, "c_actb")
        p.op("dve", lambda e: e.tensor_copy(out=c_actb[:], in_=c_act[:]),
             reads=["c_act"], writes=["c_actb"])
        c_rep = p.sb([128, KT, 128], BF16, "c_rep")
        for kt in range(KT):
            p.op("dve", lambda e, kt=kt: e.tensor_scalar(
                out=c_rep[:, kt, :], in0=ones[:], scalar1=c_act[:, kt:kt + 1], scalar2=None,
                op0=ALU.mult), reads=["c_act", "ones"], writes=[("c_rep", kt)])

        adab_fm = p.sb([128, 96], F32, "adab_fm")
        p.dma("sp", adab_fm[:], adab_fm_d, "const", writes=["adab_fm"])
        g1 = p.sb([128, KT], F32, "g1")
        g2 = p.sb([128, KT], F32, "g2")
        p.dma("sp", g1[:], g1_d, "const", writes=["g1"])
        p.dma("sp", g2[:], g2_d, "const", writes=["g2"])

        mod_fm = p.sb([128, 96], F32, "mod_fm")
        p.op("dve", lambda e: e.memset(mod_fm[:], 0.0), writes=[("mod", i) for i in range(12)])
        gate_bc = MS[:, 57344:65536].bitcast(F32)
        gate_d = nc.dram_tensor("gate_scratch", [128, 2 * D], F32, kind="Internal").ap()
        p.dma("sp", gate_bc, adab_bc_d, "const", writes=[("gate_bc", i) for i in range(8)])
        ACH = 1024
        aw = [BIG2[:, i * 16384:(i + 1) * 16384].rearrange("p (a b) -> p a b", a=KT) for i in range(2)]

        def ada_chunk(ci):
            b = ci % 2
            n0 = ci * ACH
            p.dma("pool", aw[b], adaw_d[:, ci], "aw%d" % b, writes=[("aw", b)], max_dma_last_dim=4096)
            sec = n0 // D
            if sec in (2, 5):
                for h in range(ACH // 512):
                    pi = next_ps()
                    for kt in range(KT):
                        p.op("pe", lambda e, kt=kt, pi=pi, h=h: e.matmul(
                            ps[pi][:, :], lhsT=c_rep[:, kt, :], rhs=aw[b][:, kt, h * 512:(h + 1) * 512],
                            start=(kt == 0), stop=(kt == KT - 1)),
                            reads=[("aw", b), ("c_rep", kt)], writes=[("ps", pi)])
                    g0 = (0 if sec == 2 else D) + (n0 % D) + h * 512
                    p.op("dve", lambda e, pi=pi, g0=g0: e.tensor_tensor(
                        out=gate_bc[:, g0:g0 + 512], in0=ps[pi][:, :], in1=gate_bc[:, g0:g0 + 512],
                        op=ALU.add), reads=[("ps", pi), ("gate_bc", g0 // 512)], writes=[("gate_bc", g0 // 512)])
            else:
                pi = next_ps()
                for jj in range(ACH // 128):
                    j = n0 // 128 + jj
                    for kt in range(KT):
                        p.op("pe", lambda e, kt=kt, pi=pi, jj=jj: e.matmul(
                            ps[pi][:, jj:jj + 1], lhsT=aw[b][:, kt, jj * 128:(jj + 1) * 128],
                            rhs=c_actb[:, kt:kt + 1], start=(kt == 0), stop=(kt == KT - 1)),
                            reads=[("aw", b), "c_actb"], writes=[("ps", pi)])
                j0 = n0 // 128
                nj = ACH // 128
                p.op("dve", lambda e, pi=pi, j0=j0, nj=nj: e.tensor_tensor(
                    out=mod_fm[:, j0:j0 + nj], in0=ps[pi][:, 0:nj], in1=adab_fm[:, j0:j0 + nj],
                    op=ALU.add), reads=[("ps", pi), "adab_fm"], writes=[("mod", j0 // 8)])

        A1 = p.sb([128, KT], F32, "A1")
        A2 = p.sb([128, KT], F32, "A2")

        def make_A(A, g, gname, sec_scale):
            j0 = sec_scale * 16
            p.op("dve", lambda e: e.scalar_tensor_tensor(
                out=A[:], in0=mod_fm[:, j0:j0 + 16], scalar=1.0, in1=g[:],
                op0=ALU.add, op1=ALU.mult),
                reads=[("mod", j0 // 8), ("mod", j0 // 8 + 1), gname], writes=[("A", sec_scale)])

        for ci in range(4):
            ada_chunk(ci)
        make_A(A1, g1, "g1", 1)
        if debug == "A":
            dbg_mod = nc.dram_tensor("dbg_mod", [128, 32], F32, kind="ExternalOutput").ap()
            p.dma("sp", dbg_mod, mod_fm[:, 0:32], "dbg", reads=[("mod", i) for i in range(4)])
            dbg_A = nc.dram_tensor("dbg_A", [128, 16], F32, kind="ExternalOutput").ap()
            p.dma("sp", dbg_A, A1[:], "dbg", reads=[("A", 1)])
            p.barrier()
            p.emit()
            return nc

        hT = BIG1[:].rearrange("p (a b) -> p a b", a=KT)
        mixT = BIG2.rearrange("p (a b) -> p a b", a=KT)
        xb = [MS[:, 32768 + i * 4096:32768 + (i + 1) * 4096].bitcast(F32) for i in range(4)]
        xn = MS[:, 49152:53248].bitcast(F32)
        junk = xn
        ssq = p.sb([128, 2 * NT], F32, "ssq")
        rstd = p.sb([128, 2 * NT], F32, "rstd")

        def rms_stats(src_tile, src_key, si):
            p.op("act", lambda e: e.activation(out=junk, in_=src_tile, func=AF.Square,
                                               accum_out=ssq[:, si:si + 1]),
                 reads=[src_key], writes=["xn", ("ssq", si)])
            p.op("act", lambda e: e.activation(out=rstd[:, si:si + 1], in_=ssq[:, si:si + 1], func=AF.Sqrt,
                                               scale=1.0 / D, bias=EPS),
                 reads=[("ssq", si)], writes=[("rstd", si)])
            p.op("dve", lambda e: e.reciprocal(out=rstd[:, si:si + 1], in_=rstd[:, si:si + 1]),
                 reads=[("rstd", si)], writes=[("rstd", si)])

        def norm_transpose(src_tile, src_key, si, A, Akey, shift_j0, dst, c0, dkey):
            rms_stats(src_tile, src_key, si)
            p.op("act", lambda e: e.activation(out=xn, in_=src_tile, func=AF.Copy,
                                               scale=rstd[:, si:si + 1]),
                 reads=[src_key, ("rstd", si)], writes=["xn"])
            mk = [("mod", shift_j0 // 8), ("mod", shift_j0 // 8 + 1)]
            for q4 in range(KT // 4):
                pi = next_ps()
                for i in range(4):
                    kt = q4 * 4 + i
                    p.op("pe", lambda e, kt=kt, i=i, pi=pi: e.transpose(
                        out=ps[pi][:, i * 128:(i + 1) * 128], in_=xn[:, kt * 128:(kt + 1) * 128],
                        identity=ident[:]), reads=["xn", "ident"], writes=[("ps", pi)])
                for i in range(4):
                    kt = q4 * 4 + i
                    if i % 2 == 0:
                        p.op("dve", lambda e, kt=kt, i=i, pi=pi: e.tensor_scalar(
                            out=dst[:, kt, c0:c0 + 128], in0=ps[pi][:, i * 128:(i + 1) * 128],
                            scalar1=A[:, kt:kt + 1], scalar2=mod_fm[:, shift_j0 + kt:shift_j0 + kt + 1],
                            op0=ALU.mult, op1=ALU.add),
                            reads=[("ps", pi), Akey] + mk, writes=[dkey])
                    else:
                        p.op("act", lambda e, kt=kt, i=i, pi=pi: e.activation(
                            out=dst[:, kt, c0:c0 + 128], in_=ps[pi][:, i * 128:(i + 1) * 128],
                            func=AF.Identity, scale=A[:, kt:kt + 1],
                            bias=mod_fm[:, shift_j0 + kt:shift_j0 + kt + 1]),
                            reads=[("ps", pi), Akey] + mk, writes=[dkey])

        for s in range(NT):
            b = s % 4
            p.dma("sp", xb[b], x_t[:, s, :], "xb%d" % b, writes=[("xb", b)])
            norm_transpose(xb[b], ("xb", b), s, A1, ("A", 1), 0, hT, s * 128, ("hT", s))

        for ci in range(4, 12):
            ada_chunk(ci)
        make_A(A2, g2, "g2", 4)

        if debug == "B":
            dbg_hT = nc.dram_tensor("dbg_hT", [128, KT, S], BF16, kind="ExternalOutput").ap()
            dbg_mod = nc.dram_tensor("dbg_mod", [128, 96], F32, kind="ExternalOutput").ap()
            dbg_gate = nc.dram_tensor("dbg_gate", [128, 2 * D], F32, kind="ExternalOutput").ap()
            p.dma("sp", dbg_hT, hT, "dbg", reads=[("hT", s) for s in range(NT)])
            p.dma("sp", dbg_mod, mod_fm[:], "dbg", reads=[("mod", i) for i in range(12)])
            p.dma("sp", dbg_gate, gate_bc, "dbg", reads=[("gate_bc", i) for i in range(8)])
            p.barrier()
            p.emit()
            return nc

        mixT_d = nc.dram_tensor("mixT_scratch", [128, KT, S], BF16, kind="Internal").ap()
        p.dma("sp", gate_d, gate_bc, "gatew", reads=[("gate_bc", i) for i in range(8)], writes=["gated"])
        p.barrier()
        w_in_d = din("w_in_h", [128, 36, KT, 128])
        if debug == "T":
            p.dma("sp", mixT_d, hT, "mixw", reads=[("hT", s) for s in range(NT)], writes=["mixd"])
        else:
            mixer_phase(p, nc, locals())
        p.barrier()
        if debug in ("S5", "RK"):
            dbg_mix = nc.dram_tensor("dbg_mix", [128, 8, S], BF16, kind="ExternalOutput").ap()
            k0 = 0 if debug == "S5" else 8
            p.dma("sp", dbg_mix, mixT_d[:, k0:k0 + 8, :], "dbg")
            p.barrier()
            p.emit()
            return nc

        w_out_d = din("w_out", [128, 4, KT, 512])
        x1_d = nc.dram_tensor("x1_scratch", [S, D], F32, kind="Internal").ap()
        x1_t = x1_d.rearrange("(c s) d -> c s d", s=NT)
        wo = [BIG1[:, i * 8192:(i + 1) * 8192].rearrange("p (a b) -> p a b", a=KT) for i in range(3)]
        p.dma("sp", mixT, mixT_d, "mixld", reads=["mixd"], writes=["mixT"])
        p.dma("sp", gate_bc, gate_d, "gateld", reads=["gated"], writes=["gate_bc"])
        xq = [xb[i][:, 0:512] for i in range(4)]
        tq = [xb[i][:, 512:1024] for i in range(4)]
        for nch in range(4):
            b = nch % 3
            cols = slice(nch * 512, (nch + 1) * 512)
            p.dma("pool", wo[b], w_out_d[:, nch], "wo%d" % b, writes=[("wo", b)], max_dma_last_dim=4096)
            for s in range(NT):
                pi = next_ps()
                r = s % 4
                for kt in range(KT):
                    p.op("pe", lambda e, kt=kt, pi=pi, s=s, b=b: e.matmul(
                        ps[pi][:, :], lhsT=mixT[:, kt, s * 128:(s + 1) * 128], rhs=wo[b][:, kt, :],
                        start=(kt == 0), stop=(kt == KT - 1)),
                        reads=[("wo", b), "mixT"], writes=[("ps", pi)])
                p.dma("sp", xq[r], x_t[:, s, cols], "xq%d" % r, writes=[("xq", r)])
                p.op("dve", lambda e, pi=pi, r=r, cols=cols: e.tensor_tensor(
                    out=tq[r], in0=ps[pi][:, :], in1=gate_bc[:, cols], op=ALU.mult),
                    reads=[("ps", pi), "gate_bc"], writes=[("tq", r)])
                p.op("dve", lambda e, r=r: e.tensor_tensor(
                    out=tq[r], in0=tq[r], in1=xq[r], op=ALU.add),
                    reads=[("tq", r), ("xq", r)], writes=[("tq", r)])
                p.dma("sp", x1_t[:, s, cols], tq[r], "x1w%d" % r, reads=[("tq", r)], writes=[("x1d", s)])
        p.barrier()

        w1_d = din("ffn_w1", [128, FFN // 256, KT, 256])
        w2_d = din("ffn_w2", [128, 4, 8, 8, 512])
        fg_d = din("fgain_bc", [128, D])
        fgain = MS[:, 53248:57344].bitcast(F32)
        p.dma("sp", fgain, fg_d, "fgain", writes=["fgain"])
        hidT = BIG1[:].rearrange("p (a b) -> p a b", a=FFN // 128)
        NWB = 6
        wring = [BIG2[:, 0:4096], BIG2[:, 4096:8192], MS[:, 28672:32768],
                 BIG2[:, 8192:12288], BIG2[:, 12288:16384], MS[:, 57344:61440]]
        w1c = [r_.rearrange("p (a b) -> p a b", a=KT) for r_ in wring]
        w2c = [r_.rearrange("p (a b) -> p a b", a=8) for r_ in wring]
        wcnt = [0]
        h2g = BIG2[:, 16384:16384 + KT * 512].rearrange("p (a b) -> p a b", a=KT)
        rl = [BIG2[:, 24576 + i * 1024:24576 + (i + 1) * 1024].bitcast(F32) for i in range(2)]
        tq2 = [BIG2[:, 26624 + i * 1024:26624 + (i + 1) * 1024].bitcast(F32) for i in range(2)]
        HW1 = 256
        for G in range(4):
            for i in range(4):
                s = 4 * G + i
                p.dma("sp", xb[i], x1_t[:, s, :], "xb%d" % i, reads=[("x1d", s)], writes=[("xb", i)])
                norm_transpose(xb[i], ("xb", i), NT + s, A2, ("A", 4), 48, h2g, i * 128, "h2g")
            for hc in range(FFN // HW1):
                b = wcnt[0] % NWB
                wcnt[0] += 1
                p.dma("pool", w1c[b], w1_d[:, hc], "wb%d" % b, writes=[("wb", b)], max_dma_last_dim=4096)
                for j in range(HW1 // 128):
                    ht = hc * (HW1 // 128) + j
                    pi = next_ps()
                    for kt in range(KT):
                        p.op("pe", lambda e, kt=kt, pi=pi, j=j, b=b: e.matmul(
                            ps[pi][:, :], lhsT=w1c[b][:, kt, j * 128:(j + 1) * 128], rhs=h2g[:, kt, :],
                            start=(kt == 0), stop=(kt == KT - 1)),
                            reads=[("wb", b), "h2g"], writes=[("ps", pi)])
                    rb = ht % 2
                    p.op("act", lambda e, pi=pi, rb=rb: e.activation(out=rl[rb], in_=ps[pi][:, :], func=AF.Relu),
                         reads=[("ps", pi)], writes=[("rl", rb)])
                    p.op("dve", lambda e, pi=pi, rb=rb, ht=ht: e.tensor_tensor(
                        out=hidT[:, ht, :], in0=ps[pi][:, :], in1=rl[rb], op=ALU.mult),
                        reads=[("ps", pi), ("rl", rb)], writes=[("hidT", ht)])
            for nch in range(4):
                cols = slice(nch * 512, (nch + 1) * 512)
                gcols = slice(D + nch * 512, D + (nch + 1) * 512)
                pis = [next_ps() for i in range(4)]
                for hg in range(8):
                    b = wcnt[0] % NWB
                    wcnt[0] += 1
                    p.dma("pool", w2c[b], w2_d[:, nch, hg], "wb%d" % b, writes=[("wb", b)], max_dma_last_dim=4096)
                    for i in range(4):
                        for h8 in range(8):
                            ht = hg * 8 + h8
                            p.op("pe", lambda e, i=i, h8=h8, ht=ht, b=b, pis=pis: e.matmul(
                                ps[pis[i]][:, :], lhsT=hidT[:, ht, i * 128:(i + 1) * 128], rhs=w2c[b][:, h8, :],
                                start=(ht == 0), stop=(ht == 63)),
                                reads=[("wb", b), ("hidT", ht)], writes=[("ps", pis[i])])
                for i in range(4):
                    tb = i % 2
                    p.op("dve", lambda e, i=i, tb=tb, pis=pis, gcols=gcols: e.tensor_tensor(
                        out=tq2[tb], in0=ps[pis[i]][:, :], in1=gate_bc[:, gcols], op=ALU.mult),
                        reads=[("ps", pis[i])], writes=[("tq2", tb)])
                    p.op("dve", lambda e, i=i, tb=tb, cols=cols: e.tensor_tensor(
                        out=xb[i][:, cols], in0=xb[i][:, cols], in1=tq2[tb], op=ALU.add),
                        reads=[("tq2", tb), ("xb", i)], writes=[("xb", i)])
            for i in range(4):
                s = 4 * G + i
                rms_stats(xb[i], ("xb", i), s)
                p.op("act", lambda e, i=i, s=s: e.activation(out=xn, in_=xb[i], func=AF.Copy,
                                                             scale=rstd[:, s:s + 1]),
                     reads=[("xb", i), ("rstd", s)], writes=["xn"])
                p.op("dve", lambda e, i=i: e.tensor_tensor(out=xb[i], in0=xn, in1=fgain, op=ALU.mult),
                     reads=["xn", "fgain"], writes=[("xb", i)])
                p.dma("sp", out_t[:, s, :], xb[i], "outw%d" % i, reads=[("xb", i)], writes=[("outd", s)])
        p.barrier()
        p.emit()
    return nc


I32 = mybir.dt.int32
TWO_PI = 6.283185307179586


class Arena:
    def __init__(self, MS, nbytes):
        self.MS = MS
        self.cap = nbytes
        self.off = 0

    def reset(self, off=0):
        self.off = off

    def alloc(self, parts, shape, dtype):
        n = 1
        for d_ in shape:
            n *= d_
        esz = 2 if dtype == BF16 else 4
        nb = (n * esz + 31) // 32 * 32
        o = self.off
        self.last = o // 2
        self.off += nb
        assert self.off <= self.cap, ("arena overflow", self.off, self.cap)
        ap = self.MS[0:parts, o // 2:(o + n * esz) // 2]
        if dtype != BF16:
            ap = ap.bitcast(dtype)
        if len(shape) > 1:
            names = " ".join("d%d" % i for i in range(len(shape)))
            kw = {"d%d" % i: shape[i] for i in range(1, len(shape))}
            ap = ap.rearrange("p (%s) -> p %s" % (names, names), **kw)
        return ap


def run_gens(gens):
    gens = list(gens)
    while gens:
        alive = []
        for g_ in gens:
            try:
                next(g_)
                alive.append(g_)
            except StopIteration:
                pass
        gens = alive


def bc(ap, axis, n):
    v = ap.unsqueeze(axis)
    shp = list(v.shape)
    shp[axis] = n
    return v.broadcast_to(shp)


def mixer_phase(p, nc, env):
    if env.get("debug") != "S5":
        rwkv_phase(p, nc, env)
        p.barrier()
    if env.get("debug") != "RK":
        s5_phase(p, nc, env)


CDEC = -0.6065306597126334
GN_EPS = 64e-5


def rwkv_phase(p, nc, env):
    ps, next_ps, hT, MS, din, ident = (env[k] for k in ("ps", "next_ps", "hT", "MS", "din", "ident"))
    mixT_d = env["mixT_d"]
    w_in_d = env["w_in_d"]
    ar = Arena(MS, 131072)

    prm_d = din("rk_prm", [128, 112])
    wup_d = din("rk_wup", [128, 1024])
    aup_d = din("rk_aup", [128, 1024])
    gup1_d = din("rk_gup1", [128, 1024])
    gup2_d = din("rk_gup2", [32, 1024])
    lng_d = din("rk_lng", [8, 128, 64])
    lnb_d = din("rk_lnb", [8, 128, 64])
    smask_d = din("rk_smask", [128, 512])
    mask5_d = din("rk_mask5", [128, 2, 320])
    i64_d = din("rk_i64", [128, 64])
    bones_d = din("rk_bones", [128, 128])

    def tt(eng, out, a, b, op, r, w):
        p.op(eng, lambda e: e.tensor_tensor(out=out, in0=a, in1=b, op=op), reads=r, writes=w)

    def ts(eng, out, a, s1, s2, op0, op1, r, w):
        if op1 is None:
            p.op(eng, lambda e: e.tensor_scalar(out=out, in0=a, scalar1=s1, scalar2=None, op0=op0), reads=r, writes=w)
        else:
            p.op(eng, lambda e: e.tensor_scalar(out=out, in0=a, scalar1=s1, scalar2=s2, op0=op0, op1=op1),
                 reads=r, writes=w)

    def stt(eng, out, a, sc, b, op0, op1, r, w):
        p.op(eng, lambda e: e.scalar_tensor_tensor(out=out, in0=a, scalar=sc, in1=b, op0=op0, op1=op1),
             reads=r, writes=w)

    def act(out, in_, func, r, w, **kw):
        p.op("act", lambda e: e.activation(out=out, in_=in_, func=func, **kw), reads=r, writes=w)

    def cp(eng, out, in_, r, w):
        if eng == "act":
            act(out, in_, AF.Copy, r, w)
        else:
            p.op(eng, lambda e: e.tensor_copy(out=out, in_=in_), reads=r, writes=w)

    def mm(out, lhsT, rhs, start, stop, r, w):
        p.op("pe", lambda e: e.matmul(out, lhsT=lhsT, rhs=rhs, start=start, stop=stop), reads=r, writes=w)

    prm = ar.alloc(128, [112], F32)
    c0 = ar.alloc(128, [28], F32)
    omk = ar.alloc(128, [8], F32)
    wup = ar.alloc(128, [1024], BF16)
    aup = ar.alloc(128, [1024], BF16)
    gup1 = ar.alloc(128, [1024], BF16)
    gup2 = ar.alloc(128, [1024], BF16)
    smask = ar.alloc(128, [512], BF16)
    mask5 = ar.alloc(128, [2, 320], F32)
    i64b = ar.alloc(128, [64], BF16)
    bones = ar.alloc(128, [128], F32)
    tw = ar.alloc(128, [S], BF16)
    ad = ar.alloc(128, [S], BF16)
    sg1 = ar.alloc(128, [S], BF16)
    sg2 = ar.alloc(128, [S], BF16)
    p.dma("sp", prm, prm_d, "rkc0", writes=["prm"])
    p.dma("sp", mask5, mask5_d, "rkc1", writes=["mask5"])
    p.dma("sp", bones, bones_d, "rkc2", writes=["bones"])
    p.dma("pool", wup, wup_d, "rkc3", writes=["wup"])
    p.dma("pool", aup, aup_d, "rkc4", writes=["aup"])
    p.dma("pool", gup1, gup1_d, "rkc5", writes=["gup1"])
    p.dma("pool", gup2[0:32, :], gup2_d, "rkc6", writes=["gup2"])
    p.dma("pool", smask, smask_d, "rkc7", writes=["smask"])
    p.dma("pool", i64b, i64_d, "rkc8", writes=["i64b"])
    MUP, MUN, W0, A0, KK, KA, RK = 0, 28, 56, 72, 88, 96, 104
    tt("dve", c0, prm[:, MUP:MUP + 28], prm[:, MUN:MUN + 28], ALU.add, ["prm"], ["c0"])
    ts("dve", c0, c0, -1.0, 1.0, ALU.mult, ALU.add, ["c0"], ["c0"])
    ts("dve", omk, prm[:, KA:KA + 8], -1.0, 1.0, ALU.mult, ALU.add, ["prm"], ["omk"])

    pn = ar.alloc(128, [S + 2], F32)
    wic = ar.alloc(128, [KT, 128], BF16)
    rS = ar.alloc(128, [S], F32)
    rS_off = ar.last
    kS = ar.alloc(128, [S], F32)
    assert ar.last == rS_off + 4096
    kap = ar.alloc(128, [S], F32)
    v_bf = ar.alloc(128, [S], BF16)
    g_bf = ar.alloc(128, [S], BF16)
    tl = [[None] * 4 for _ in range(2)]
    tl_off = [[0] * 4 for _ in range(2)]
    QR = [None, None]
    for d_ in range(2):
        QR[d_] = ar.alloc(128, [32, 2, 64], BF16)
        qr_off = ar.last
        tl_off[d_][0] = qr_off
        for a_ in (1, 2):
            tl[d_][a_] = ar.alloc(128, [S], BF16)
            tl_off[d_][a_] = ar.last
    tmp_off = tl_off[1][0]
    ynb_flat = [MS[:, tl_off[0][0]:tl_off[0][0] + 2048]]
    Gam = ar.alloc(128, [2, 32], F32)
    QW = 512
    NQ = S // QW
    CH = QW // 64
    Tset = [[ar.alloc(128, [QW], F32) for _ in range(4)], None]
    TE = ar.alloc(128, [QW], F32)
    tot8 = ar.alloc(128, [2, CH], F32)

    def run_interleaved0(gens):
        gens = list(gens)
        while gens:
            alive = []
            for g_ in gens:
                try:
                    next(g_)
                    alive.append(g_)
                except StopIteration:
                    pass
            gens = alive
    lng = ar.alloc(128, [64], F32)
    lnb = ar.alloc(128, [64], F32)
    st = ar.alloc(128, [4, 32], F32)
    CM = [ar.alloc(128, [2, 320], BF16)]
    cm_off = ar.last
    CM.append(ar.alloc(128, [2, 320], BF16))
    TT = [ar.alloc(128, [2, 192], BF16) for _ in range(2)]
    LINV = [ar.alloc(128, [2, 64], BF16) for _ in range(2)]
    IBS = [[ar.alloc(128, [2, 192], F32) for _ in range(2)]]
    pnb = pn.bitcast(BF16)
    o_ = 0
    for _ in range(2):
        CM.append(pnb[:, o_:o_ + 640].rearrange("p (d q) -> p d q", d=2)); o_ += 640
    for _ in range(2):
        TT.append(pnb[:, o_:o_ + 384].rearrange("p (d q) -> p d q", d=2)); o_ += 384
    for _ in range(2):
        LINV.append(pnb[:, o_:o_ + 128].rearrange("p (d q) -> p d q", d=2)); o_ += 128
    ib2 = []
    for _ in range(2):
        ib2.append(pnb[:, o_:o_ + 768].bitcast(F32).rearrange("p (d q) -> p d q", d=2)); o_ += 768
    IBS.append(ib2)
    assert o_ <= 4100
    Xn = ar.alloc(128, [2, 64], BF16)
    Ub = ar.alloc(128, [2, 64], BF16)
    H32 = ar.alloc(128, [2, 64], F32)
    Hb = ar.alloc(128, [2, 64], BF16)
    tmpH = ar.alloc(128, [2, 64], F32)
    mixt = tl[1][1]
    assert ar.off // 2 - cm_off >= 4 * QW * 2
    Tset[1] = [MS[:, cm_off + i_ * 2 * QW:cm_off + (i_ + 1) * 2 * QW].bitcast(F32) for i_ in range(4)]
    set1k = [("TA", 1), ("TB", 1), ("TC", 1), ("TD", 1)]
    smallk = [(n_, hp_) for n_ in ("Xn", "U", "H", "Hb", "tmpH") for hp_ in range(2)]

    p.op("pool", lambda e: e.memset(pn[:, 0:1], 0.0), writes=["pn"])
    p.op("pool", lambda e: e.memset(pn[:, S + 1:S + 2], 0.0), writes=["pn"])

    def proj_fm(col0, ncols, tidx, dst, dkey):
        p.dma("pool", wic, w_in_d[:, col0 // 128], "rkwic", writes=["wic"], max_dma_last_dim=4096)
        pn_nat = pn[0:ncols, 1:S + 1].rearrange("p (c s) -> p s c", s=16)
        for tb in range(4):
            pi = next_ps()
            for kt in range(KT):
                mm(ps[pi][0:ncols, :], wic[:, kt, 0:ncols], hT[:, kt, tb * 512:(tb + 1) * 512],
                   kt == 0, kt == KT - 1, ["wic"], [("ps", pi)])
            act(pn_nat[:, tb * 4:(tb + 1) * 4, :], ps[pi][0:ncols, :].rearrange("p (s c) -> p s c", s=4), AF.Copy,
                [("ps", pi)], ["pn"])
        act(dst[0:ncols, :], pn[0:ncols, 1:S + 1], AF.Copy, ["pn", "c0"], [dkey], scale=c0[0:ncols, tidx:tidx + 1])
        stt("dve", dst[0:ncols, :], pn[0:ncols, 0:S], prm[0:ncols, MUP + tidx:MUP + tidx + 1], dst[0:ncols, :],
            ALU.mult, ALU.add, ["pn", "prm", dkey], [dkey])
        stt("dve", dst[0:ncols, :], pn[0:ncols, 2:S + 2], prm[0:ncols, MUN + tidx:MUN + tidx + 1], dst[0:ncols, :],
            ALU.mult, ALU.add, ["pn", "prm", dkey], [dkey])

    proj_fm(4096, 128, 24, rS, "rS")
    act(tw, rS, AF.Tanh, ["rS"], ["tw"])
    proj_fm(4224, 128, 25, rS, "rS")
    cp("act", ad, rS, ["rS"], ["ad"])
    proj_fm(4352, 128, 26, rS, "rS")
    act(sg1, rS, AF.Sigmoid, ["rS"], ["sg1"])
    proj_fm(4480, 32, 27, rS, "rS")
    act(sg2[0:32, :], rS[0:32, :], AF.Sigmoid, ["rS"], ["sg2"])

    yT = [None]

    for j in range(8):
        js = slice(j * 128, (j + 1) * 128)
        p.dma("sp", lng, lng_d[j], "rklng", writes=["lng"])
        p.dma("sp", lnb, lnb_d[j], "rklnb", writes=["lnb"])
        proj_fm(1024 + j * 128, 128, j, rS, "rS")
        proj_fm(2048 + j * 128, 128, 8 + j, kS, "kS")
        proj_fm(3072 + j * 128, 128, 16 + j, kap, "kap")
        cp("act", v_bf, kap, ["kap"], ["v_bf"])
        kapk = ["kap"] + [("kap", q_) for q_ in range(NQ)]
        ts("dve", kap, kS, prm[:, KK + j:KK + j + 1], None, ALU.mult, None, ["kS", "prm", "v_bf"], kapk)
        pending = [None]
        for q in range(NQ):
            qs = slice(q * QW, (q + 1) * QW)
            TA0, TB0 = Tset[0][0], Tset[0][1]
            tt("pool", TA0, kap[:, qs], kap[:, qs], ALU.mult, [("kap", q)], [("TA", 0)])
            pi = next_ps()
            mm(ps[pi][:, 0:QW], bones, TA0, True, True, ["bones", ("TA", 0)], [("ps", pi)])
            act(TB0, ps[pi][:, 0:QW], AF.Sqrt, [("ps", pi)], [("TB", 0)])
            ts("dve", TB0, TB0, 1e-12, None, ALU.max, None, [("TB", 0)], [("TB", 0)])
            p.op("dve", lambda e, TB0=TB0: e.reciprocal(out=TB0, in_=TB0), reads=[("TB", 0)], writes=[("TB", 0)])
            tt("dve", kap[:, qs], kap[:, qs], TB0, ALU.mult, [("kap", q), ("TB", 0)], [("kap", q)])

            def chain(d, q=q, qs=qs):
                TA, TB, TC, TD = Tset[d]
                kA, kB, kC, kD = ("TA", d), ("TB", d), ("TC", d), ("TD", d)
                ds_ = slice(d * 64, (d + 1) * 64)
                pi = next_ps()
                mm(ps[pi][:, 0:QW], wup[ds_, js], tw[ds_, qs], True, True, ["wup", "tw"], [("ps", pi)])
                act(TA, ps[pi][:, 0:QW], AF.Sigmoid, [("ps", pi), "prm"], [kA],
                    bias=prm[:, W0 + d * 8 + j:W0 + d * 8 + j + 1])
                yield
                pi = next_ps()
                mm(ps[pi][:, 0:QW], aup[ds_, js], ad[ds_, qs], True, True, ["aup", "ad"], [("ps", pi)])
                act(TB, ps[pi][:, 0:QW], AF.Sigmoid, [("ps", pi), "prm"], [kB],
                    bias=prm[:, A0 + d * 8 + j:A0 + d * 8 + j + 1])
                yield
                p.op("dve", lambda e: e.tensor_tensor_scan(out=TD, data0=smask[:, 0:QW], data1=TA, initial=0.0,
                                                            op0=ALU.mult, op1=ALU.add),
                     reads=["smask", kA], writes=[kD])
                yield
                ts("dve", TC, TB, prm[:, KA + j:KA + j + 1], omk[:, j:j + 1], ALU.mult, ALU.add,
                   [kB, "prm", "omk"], [kC])
                yield
                tt("pool", TC, TC, kS[:, qs], ALU.mult, [kC, "kS"], [kC])
                yield
                tt("pool", TB, TB, kap[:, qs], ALU.mult, [kB, ("kap", q)], [kB])
                yield
                TD3 = TD.rearrange("p (c t) -> p c t", t=64)
                TA3 = TA.rearrange("p (c t) -> p c t", t=64)
                if d == 1:
                    cp("dve", tot8[:, d, :], TD3[:, :, 63], [kD], [("tot8", d)])
                    yield
                    tt("dve", TD3, bc(tot8[:, d, :], 2, 64), TD3, ALU.subtract, [("tot8", d), kD], [kD])
                    yield
                    tt("dve", TD, TD, TA, ALU.add, [kD, kA], [kD])
                    yield
                tt("dve", TA, TD, TA, ALU.subtract, [kD, kA], [kA])
                yield
                act(TA, TA, AF.Exp, [kA], [kA], scale=CDEC)
                yield
                tt("dve", QR[d][:, q * CH:(q + 1) * CH, 0, :], kap[:, qs].rearrange("p (c t) -> p c t", t=64),
                   TA3, ALU.mult, [("kap", q), kA], [("tl", d, 0)])
                yield
                act(TA, TD, AF.Exp, [kD], [kA], scale=CDEC)
                yield
                tt("dve", QR[d][:, q * CH:(q + 1) * CH, 1, :], rS[:, qs].rearrange("p (c t) -> p c t", t=64),
                   TA3, ALU.mult, ["rS", kA], [("tl", d, 3)])
                cp("pool", Gam[:, d, q * CH:(q + 1) * CH], TA3[:, :, 63 if d == 0 else 0], [kA], ["Gam"])
                yield
                act(TA, TD, AF.Exp, [kD], [kA], scale=-CDEC)
                yield
                tt("dve", tl[d][1][:, qs], TB, TA, ALU.mult, [kB, kA], [("tl", d, 1)])
                tt("pool", tl[d][2][:, qs], TC, TA, ALU.mult, [kC, kA], [("tl", d, 2)])
                yield

            gens_ = [chain(0), chain(1)]
            if pending[0] is not None:
                gens_.append(pending[0])
            run_interleaved0(gens_)
            tt("pool", TE, Tset[0][2], Tset[1][2], ALU.add, [("TC", 0), ("TC", 1)], ["TE"])

            def misc(q=q, qs=qs):
                tt("dve", TE, rS[:, qs], TE, ALU.mult, ["rS", "TE"], ["TE"])
                yield
                ts("dve", TE, TE, prm[:, RK + j:RK + j + 1], None, ALU.mult, None, ["TE", "prm"], ["TE"])
                yield
                pi = next_ps()
                mm(ps[pi][:, 0:QW], bones, TE, True, True, ["bones", "TE"], [("ps", pi)])
                tt("dve", kap[:, qs], ps[pi][:, 0:QW], v_bf[:, qs], ALU.mult,
                   [("ps", pi), "v_bf", ("kap", q)], [("kap", q)])
                yield
                pi = next_ps()
                mm(ps[pi][:, 0:QW], gup1[:, js], sg1[:, qs], True, False, ["gup1", "sg1"], [("ps", pi)])
                mm(ps[pi][:, 0:QW], gup2[0:32, js], sg2[0:32, qs], False, True, ["gup2", "sg2"], [("ps", pi)])
                cp("act", g_bf[:, qs], ps[pi][:, 0:QW], [("ps", pi)], ["g_bf"])
                yield

            pending[0] = misc()
        run_interleaved0([pending[0]])
        pending[0] = None
        yTMv = MS[:, rS_off:rS_off + 8192].bitcast(F32).rearrange("p (d n v) -> p d n v", d=2, n=32)
        p.op("pool", lambda e: e.memset(H32.rearrange("p a b -> p (a b)"), 0.0), reads=set1k + ["TE"],
             writes=[("H", 0), ("H", 1)])
        p.op("pool", lambda e: e.memset(Hb.rearrange("p a b -> p (a b)"), 0.0), reads=set1k + ["TE"],
             writes=[("Hb", 0), ("Hb", 1)])
        tlk = [("tl", d_, a_) for d_ in range(2) for a_ in range(4)]

        def stage_P(i, hp):
            par = i % 4
            IB = IBS[i % 2]
            ibk = i % 2
            pb = hp * 64
            fs = slice(pb, pb + 64)
            nn = [i, 31 - i]
            for d in range(2):
                t_ = slice(nn[d] * 64, nn[d] * 64 + 64)
                Bt, Kt = tl[d][1][fs, t_], tl[d][2][fs, t_]
                Q = QR[d][fs, nn[d], 0, :]
                QRf = QR[d][fs, nn[d]].rearrange("p a t -> p (a t)")
                pi = next_ps()
                mm(ps[pi][fs, 0:128], Bt, QRf, True, True, tlk, [("ps", pi)])
                mm(ps[pi][fs, 128:256], Kt, QRf, True, True, tlk, [("ps", pi)])
                mm(ps[pi][fs, 256:320], Q, Bt, True, True, tlk, [("ps", pi)])
                tt("dve", CM[par][fs, d, :], ps[pi][fs, 0:320], mask5[fs, d, :], ALU.mult,
                   [("ps", pi), "mask5"], [("CM", par, hp)])
            pi = next_ps()
            pbv = ps[pi][:, :].bitcast(BF16)
            for d in range(2):
                t_ = slice(nn[d] * 64, nn[d] * 64 + 64)
                for c_, src in enumerate((tl[d][1][fs, t_], tl[d][2][fs, t_], v_bf[fs, t_])):
                    o = (d * 3 + c_) * 64
                    p.op("pe", lambda e, o=o, src=src, pbv=pbv, fs=fs: e.transpose(
                        out=pbv[fs, o:o + 64], in_=src, identity=i64b[fs, :]),
                        reads=tlk + ["v_bf", "i64b"], writes=[("ps", pi)])
            cp("act", TT[par][fs, :, :], pbv[fs, 0:384].rearrange("p (d q) -> p d q", d=2),
               [("ps", pi)], [("TT", par, hp)])
            tt("pool", IB[0][fs, :, 0:64], CM[par][fs, :, 0:64], bc(i64b[fs, :], 1, 2), ALU.add,
               [("CM", par, hp), "i64b"], [("IB", ibk, 0, hp)])
            cp("pool", IB[0][fs, :, 64:128], CM[par][fs, :, 0:64], [("CM", par, hp)], [("IB", ibk, 0, hp)])
            cp("pool", IB[0][fs, :, 128:192], CM[par][fs, :, 256:320], [("CM", par, hp)], [("IB", ibk, 0, hp)])
            yield
            cur = 0
            for lev in range(6):
                nxt = 1 - cur
                pi = next_ps()
                pv = ps[pi][fs, 0:384].rearrange("p (d q) -> p d q", d=2)
                for d in range(2):
                    o = d * 192
                    P_, M_, MT_ = IB[cur][fs, d, 0:64], IB[cur][fs, d, 64:128], IB[cur][fs, d, 128:192]
                    if lev == 0:
                        mm(ps[pi][fs, o + 64:o + 128], MT_, M_, True, True, [("IB", ibk, cur, hp)], [("ps", pi)])
                        mm(ps[pi][fs, o + 128:o + 192], M_, MT_, True, True, [("IB", ibk, cur, hp)], [("ps", pi)])
                    elif lev < 5:
                        mm(ps[pi][fs, o:o + 128], MT_, IB[cur][fs, d, 0:128], True, True, [("IB", ibk, cur, hp)], [("ps", pi)])
                        mm(ps[pi][fs, o + 128:o + 192], M_, MT_, True, True, [("IB", ibk, cur, hp)], [("ps", pi)])
                    else:
                        mm(ps[pi][fs, o:o + 64], MT_, P_, True, True, [("IB", ibk, cur, hp)], [("ps", pi)])
                if lev == 0:
                    cp("pool", IB[nxt][fs, :, 0:64], IB[cur][fs, :, 0:64], [("IB", ibk, cur, hp)], [("IB", ibk, nxt, hp)])
                    cp("act", IB[nxt][fs, :, 64:192], pv[:, :, 64:192], [("ps", pi)], [("IB", ibk, nxt, hp)])
                elif lev < 5:
                    tt("dve", IB[nxt][fs, :, 0:64], IB[cur][fs, :, 0:64], pv[:, :, 0:64], ALU.add,
                       [("IB", ibk, cur, hp), ("ps", pi)], [("IB", ibk, nxt, hp)])
                    cp("act", IB[nxt][fs, :, 64:192], pv[:, :, 64:192], [("ps", pi)], [("IB", ibk, nxt, hp)])
                else:
                    tt("dve", LINV[par][fs, :, :], IB[cur][fs, :, 0:64], pv[:, :, 0:64], ALU.add,
                       [("IB", ibk, cur, hp), ("ps", pi)], [("LINV", par, hp)])
                cur = nxt
                yield

        def stage_C(i, hp):
            par = i % 4
            pb = hp * 64
            fs = slice(pb, pb + 64)
            nn = [i, 31 - i]
            tsl = [slice(n_ * 64, n_ * 64 + 64) for n_ in nn]
            kCM, kTT, kLI = ("CM", par, hp), ("TT", par, hp), ("LINV", par, hp)
            pi = next_ps()
            for d in range(2):
                o = ps[pi][fs, d * 64:(d + 1) * 64]
                mm(o, QR[d][fs, nn[d], 0, :], Hb[fs, d, :], True, False, tlk + [("Hb", hp)], [("ps", pi)])
                mm(o, CM[par][fs, d, 128:192], TT[par][fs, d, 128:192], False, True, [kCM, kTT], [("ps", pi)])
            act(Xn[fs, :, :], ps[pi][fs, 0:128].rearrange("p (d q) -> p d q", d=2), AF.Copy,
                [("ps", pi)], [("Xn", hp)], scale=-1.0)
            yield
            pi = next_ps()
            for d in range(2):
                mm(ps[pi][fs, d * 64:(d + 1) * 64], LINV[par][fs, d, :], Xn[fs, d, :], True, True,
                   [kLI, ("Xn", hp)], [("ps", pi)])
            cp("dve", Ub[fs, :, :], ps[pi][fs, 0:128].rearrange("p (d q) -> p d q", d=2), [("ps", pi)], [("U", hp)])
            yield
            pi = next_ps()
            for d in range(2):
                o = ps[pi][fs, d * 64:(d + 1) * 64]
                mm(o, QR[d][fs, nn[d], 1, :], Hb[fs, d, :], True, False, tlk + [("Hb", hp)], [("ps", pi)])
                mm(o, CM[par][fs, d, 192:256], TT[par][fs, d, 128:192], False, False, [kCM, kTT], [("ps", pi)])
                mm(o, CM[par][fs, d, 64:128], Ub[fs, d, :], False, True, [kCM, ("U", hp)], [("ps", pi)])
            cp("act", yTMv[fs, 0, nn[0], :], ps[pi][fs, 0:64], [("ps", pi)], [("yTM", hp)])
            cp("dve", yTMv[fs, 1, nn[1], :], ps[pi][fs, 64:128], [("ps", pi)], [("yTM", hp)])
            pi = next_ps()
            for d in range(2):
                o = ps[pi][fs, d * 64:(d + 1) * 64]
                mm(o, TT[par][fs, d, 0:64], Ub[fs, d, :], True, False, [kTT, ("U", hp)], [("ps", pi)])
                mm(o, TT[par][fs, d, 64:128], TT[par][fs, d, 128:192], False, True, [kTT], [("ps", pi)])
            tt("dve", tmpH[fs, :, :], ps[pi][fs, 0:128].rearrange("p (d q) -> p d q", d=2), H32[fs, :, :], ALU.add,
               [("ps", pi), ("H", hp)], [("tmpH", hp)])
            for d in range(2):
                ts("dve" if d == 0 else "pool", H32[fs, d, :], tmpH[fs, d, :], Gam[fs, d, nn[d]:nn[d] + 1], None,
                   ALU.mult, None, [("tmpH", hp), "Gam"], [("H", hp)])
            cp("act", Hb[fs, :, :], H32[fs, :, :], [("H", hp)], [("Hb", hp)])
            yield

        p.op("pool", lambda e: e.memset(yTMv[:, 0, 0, 0:1], 0.0), reads=["rS", "kS"] + tlk,
             writes=["rS", "kS", ("yTM", 0), ("yTM", 1)])
        def run_interleaved(gens):
            gens = list(gens)
            while gens:
                alive = []
                for g_ in gens:
                    try:
                        next(g_)
                        alive.append(g_)
                    except StopIteration:
                        pass
                gens = alive

        def chain_C(i0, hp):
            for i_ in (i0, i0 + 1):
                yield from stage_C(i_, hp)

        allk = [(n_, par_, hp_) for n_ in ("CM", "TT", "LINV") for par_ in range(4) for hp_ in range(2)] + \
               [("IB", a_, b_, hp_) for a_ in range(2) for b_ in range(2) for hp_ in range(2)]
        p.op("pool", lambda e: e.memset(pnb[:, 0:2], 0.0), reads=["pn", "TE"] + set1k,
             writes=allk + ["pn"] + [k_ for k_ in smallk if k_[0] in ("Xn", "U", "tmpH")])
        run_interleaved([stage_P(0, 0), stage_P(0, 1), stage_P(1, 0), stage_P(1, 1)])
        for i in range(0, 32, 2):
            gens = [chain_C(i, 0), chain_C(i, 1)]
            if i + 2 < 32:
                gens += [stage_P(i + 2, 0), stage_P(i + 2, 1), stage_P(i + 3, 0), stage_P(i + 3, 1)]
            run_interleaved(gens)
        p.op("pool", lambda e: e.memset(pn[:, 0:1], 0.0), reads=allk + smallk, writes=["pn"] + allk + set1k)

        y0 = yTMv[:, 0]
        y1 = yTMv[:, 1]
        yk = [("yTM", 0), ("yTM", 1), "rS", "kS"]
        tt("dve", y0, y0, y1, ALU.add, yk, yk)
        p.op("dve", lambda e: e.tensor_reduce(out=st[:, 0, :], in_=y0, axis=AX.X, op=ALU.add), reads=yk, writes=["st"])
        ts("dve", st[:, 0, :], st[:, 0, :], 1.0 / 64, None, ALU.mult, None, ["st"], ["st"])
        tt("dve", y0, y0, bc(st[:, 0, :], 2, 64), ALU.subtract, yk + ["st"], yk)
        tt("pool", y1, y0, y0, ALU.mult, yk, yk)
        p.op("dve", lambda e: e.tensor_reduce(out=st[:, 1, :], in_=y1, axis=AX.X, op=ALU.add), reads=yk, writes=["st"])
        act(st[:, 1, :], st[:, 1, :], AF.Sqrt, ["st"], ["st"], scale=1.0 / 64, bias=GN_EPS)
        p.op("dve", lambda e: e.reciprocal(out=st[:, 1, :], in_=st[:, 1, :]), reads=["st"], writes=["st"])
        tt("dve", y0, y0, bc(st[:, 1, :], 2, 64), ALU.mult, yk + ["st"], yk)
        tt("dve", y0, y0, bc(lng, 1, 32), ALU.mult, yk + ["lng"], yk)
        ynb = ynb_flat[0].rearrange("p (n v) -> p n v", n=32)
        tt("dve", ynb, y0, bc(lnb, 1, 32), ALU.add, yk + ["lnb"] + tlk, [("tl", 0, 0)])
        mix_nat = mixt.rearrange("p (s c) -> p c s", s=16)
        for hp in range(2):
            fs = slice(hp * 64, hp * 64 + 64)
            for h2 in range(2):
                pi = next_ps()
                pbv = ps[pi][:, :].bitcast(BF16)
                for n8 in range(16):
                    n_ = h2 * 16 + n8
                    p.op("pe", lambda e, n_=n_, n8=n8, pbv=pbv, fs=fs: e.transpose(
                        out=pbv[fs, n8 * 64:(n8 + 1) * 64], in_=ynb[fs, n_, :], identity=i64b[fs, :]),
                        reads=[("tl", 0, 0), "i64b"], writes=[("ps", pi)])
                tks = slice(h2 * 1024, (h2 + 1) * 1024)
                tmpf = MS[fs, tmp_off:tmp_off + 2048].bitcast(F32)
                tt("dve", tmpf, pbv[fs, :], kap[fs, tks], ALU.add, [("ps", pi)] + kapk, [("tmpf", hp), ("tl", 1, 0)])
                tt("pool", mix_nat[fs, h2 * 64:(h2 + 1) * 64, :],
                   tmpf.rearrange("p (c s) -> p c s", s=16), g_bf[fs, tks].rearrange("p (c s) -> p c s", s=16),
                   ALU.mult, [("tmpf", hp), ("tl", 1, 0), "g_bf"], ["mixt", ("tl", 1, 0), ("tl", 1, 1)])
        p.dma("sp", mixT_d[:, 8 + j, :], mixt, "rkmix", reads=["mixt", ("tl", 1, 1)], writes=["mixd"])


def s5_phase(p, nc, env):
    ps, next_ps, hT, MS, din, ident = (env[k] for k in ("ps", "next_ps", "hT", "MS", "din", "ident"))
    mixT_d = env["mixT_d"]
    ar = Arena(MS, 131072)

    w_in_d = env["w_in_d"]
    lamP_d = din("s5_lamP", [8, 64, 3, 2, 8])
    bP_d = din("s5_bP", [8, 64, 2, 8, 16])
    cP_d = din("s5_cP", [8, 64, 2, 8, 16])
    E_d = din("s5_E", [64, 2, 49])
    E2_d = din("s5_E2pi", [64, 2, 49])
    mask_d = din("s5_mask", [128, 2, 2, 256])
    dbc_d = din("s5_d_bc", [128, 1024])
    cidx_d = din("s5_cidx", [64, 2, 128])
    rmask_d = din("s5_rmask", [64, 2, 130])
    wglu_d = din("w_glu_h", [128, 8, 1024])
    bglu_d = din("b_glu_fm", [128, 8])
    ygT_d = nc.dram_tensor("ygT_scratch", [8, 128, S], BF16, kind="Internal").ap()

    E_sb = p.sb([64, 2, 49], F32, "s5E")
    E2_sb = p.sb([64, 2, 49], F32, "s5E2")
    mask_sb = p.sb([128, 2, 2, 256], F32, "s5mask")
    identb = p.sb([128, 128], BF16, "identb")
    bglu = p.sb([128, 8], F32, "bglu")
    cidx = p.sb([64, 2, 128], F32, "s5cidx")
    rmask = p.sb([64, 2, 130], F32, "s5rmask")
    p.dma("sp", cidx[:], cidx_d, "const2", writes=["cidx"])
    p.dma("sp", rmask[:], rmask_d, "const2", writes=["rmask"])
    p.dma("sp", E_sb[:], E_d, "const2", writes=["s5E"])
    p.dma("sp", E2_sb[:], E2_d, "const2", writes=["s5E2"])
    p.dma("sp", mask_sb[:], mask_d, "const2", writes=["s5mask"])
    p.dma("sp", bglu[:], bglu_d, "const2", writes=["bglu"])
    p.op("dve", lambda e: e.tensor_copy(out=identb[:], in_=ident[:]), reads=["ident"], writes=["identb"])

    wic = [ar.alloc(128, [KT, 128], BF16) for _ in range(2)]
    u_tm = ar.alloc(128, [8, 16, 16], BF16)
    uT = ar.alloc(128, [8, 2, 128], BF16)
    TgT = ar.alloc(128, [2, 8, 256], BF16)
    WmT = ar.alloc(128, [2, 2, 2, 8, 64], BF16)
    wmt_off = ar.last
    Wsb_f = ar.alloc(64, [4160], F32)
    Wsb = Wsb_f.rearrange("p (d r g c) -> p d r g c", d=2, r=2, g=8)
    Y_f = ar.alloc(64, [8704], BF16)
    Ytab = Y_f.rearrange("p (d r g n h) -> p d r g n h", d=2, r=2, g=8, n=17)
    Yf = Y_f.rearrange("p (d r g q) -> p d r g q", d=2, r=2, g=8)
    XW_f = ar.alloc(64, [8192], BF16)
    XWt = XW_f.rearrange("p (d r g m h) -> p d r g m h", d=2, r=2, g=8, m=16)
    XWf = XW_f.rearrange("p (d r g q) -> p d r g q", d=2, r=2, g=8)
    HX_f = XW_f[:, 0:4160]
    HX = HX_f.rearrange("p (d r g c) -> p d r g c", d=2, r=2, g=8)
    T1 = ar.alloc(128, [1600], F32)
    T2 = ar.alloc(128, [1600], F32)
    Lre = ar.alloc(64, [2, 8, 49], F32)
    Lim = ar.alloc(64, [2, 8, 49], F32)
    bbar = ar.alloc(64, [2, 2, 8, 16], F32)
    bP = ar.alloc(64, [2, 8, 16], F32)
    cP = ar.alloc(64, [2, 8, 16], F32)
    lamP = ar.alloc(64, [3, 2, 8], F32)
    sm = ar.alloc(64, [12, 2, 8], F32)
    dbc = ar.alloc(128, [128], F32)
    r16 = ar.alloc(64, [2, 8], F32)
    m16 = ar.alloc(64, [2, 8], F32)
    Mdec = ar.alloc(64, [2, 8, 130], F32)
    TWC = ar.alloc(64, [8, 128], F32)
    TWS = ar.alloc(64, [8, 128], F32)

    p.op("pool", lambda e: e.memset(Wsb_f, 0.0), writes=["Wsb"])

    def T1v(parts, shape):
        n = 1
        for d_ in shape:
            n *= d_
        v = T1[0:parts, 0:n]
        names = " ".join("d%d" % i for i in range(len(shape)))
        kw = {"d%d" % i: shape[i] for i in range(1, len(shape))}
        return v.rearrange("p (%s) -> p %s" % (names, names), **kw) if len(shape) > 1 else v

    def T2v(parts, shape):
        n = 1
        for d_ in shape:
            n *= d_
        v = T2[0:parts, 0:n]
        names = " ".join("d%d" % i for i in range(len(shape)))
        kw = {"d%d" % i: shape[i] for i in range(1, len(shape))}
        return v.rearrange("p (%s) -> p %s" % (names, names), **kw) if len(shape) > 1 else v

    def tt(eng, out, a, b, op, r, w):
        p.op(eng, lambda e: e.tensor_tensor(out=out, in0=a, in1=b, op=op), reads=r, writes=w)

    def ts(eng, out, a, s1, s2, op0, op1, r, w):
        if op1 is None:
            p.op(eng, lambda e: e.tensor_scalar(out=out, in0=a, scalar1=s1, scalar2=None, op0=op0), reads=r, writes=w)
        else:
            p.op(eng, lambda e: e.tensor_scalar(out=out, in0=a, scalar1=s1, scalar2=s2, op0=op0, op1=op1),
                 reads=r, writes=w)

    def act(out, in_, func, r, w, **kw):
        p.op("act", lambda e: e.activation(out=out, in_=in_, func=func, **kw), reads=r, writes=w)

    for gb in range(8):
        p.dma("sp", lamP, lamP_d[gb], "s5p0", writes=["lamP"])
        p.dma("sp", bP, bP_d[gb], "s5p1", writes=["bP"])
        p.dma("sp", cP, cP_d[gb], "s5p2", writes=["cP"])
        p.dma("sp", dbc, dbc_d[:, gb * 128:(gb + 1) * 128], "s5p3", writes=["dbc"])
        b = gb % 2
        p.dma("pool", wic[b], w_in_d[:, gb], "wic%d" % b, writes=[("wic", b)], max_dma_last_dim=4096)
        for s4 in range(4):
            pi = next_ps()
            for i in range(4):
                s_ = s4 * 4 + i
                for kt in range(KT):
                    p.op("pe", lambda e, kt=kt, pi=pi, i=i, s_=s_, b=b: e.matmul(
                        ps[pi][:, i * 128:(i + 1) * 128], lhsT=hT[:, kt, s_ * 128:(s_ + 1) * 128],
                        rhs=wic[b][:, kt, :], start=(kt == 0), stop=(kt == KT - 1)),
                        reads=[("wic", b)], writes=[("ps", pi)])
            act(u_tm[:, :, s4 * 4:(s4 + 1) * 4, :].rearrange("p g s h -> p s g h"),
                ps[pi][:, :].rearrange("p (s g h) -> p s g h", s=4, g=8), AF.Copy,
                [("ps", pi)], ["u_tm"])
        for half in range(2):
            pi = next_ps()
            pb = ps[pi][:, :].bitcast(BF16)
            for g in range(8):
                p.op("pe", lambda e, g=g, half=half, pb=pb: e.transpose(
                    out=pb[:, g * 128:(g + 1) * 128],
                    in_=u_tm[:, g, half * 8:(half + 1) * 8, :].rearrange("p s h -> p (s h)"),
                    identity=identb[:]), reads=["u_tm", "identb"], writes=[("ps", pi)])
            p.op("dve", lambda e, half=half, pb=pb: e.tensor_copy(
                out=uT[:, :, half, :], in_=pb.rearrange("p (g c) -> p g c", g=8)),
                reads=[("ps", pi)], writes=["uT"])
        lr, li, lst = lamP[:, 0], lamP[:, 1], lamP[:, 2]
        step, lrs, th = sm[:, 0], sm[:, 1], sm[:, 2]
        act(step, lst, AF.Exp, ["lamP"], ["sm0"])
        tt("dve", lrs, lr, step, ALU.mult, ["lamP", "sm0"], ["sm1"])
        tt("dve", th, li, step, ALU.mult, ["lamP", "sm0"], ["sm2"])
        ang = T1v(64, [2, 8, 49])
        lgm = T2v(64, [2, 8, 49])
        qi = T1[0:64, 784:784 + 784].bitcast(I32).rearrange("p (a b c) -> p a b c", a=2, b=8)
        qf = T2[0:64, 784:784 + 784].rearrange("p (a b c) -> p a b c", a=2, b=8)
        tt("dve", ang, bc(th, 3, 49), bc(E2_sb[:], 2, 8), ALU.mult, ["sm2", "s5E2"], ["T1"])
        tt("dve", lgm, bc(lrs, 3, 49), bc(E_sb[:], 2, 8), ALU.mult, ["sm1", "s5E"], ["T2"])
        p.op("dve", lambda e: e.tensor_copy(out=qi, in_=ang), reads=["T1"], writes=["T1"])
        p.op("dve", lambda e: e.tensor_copy(out=qf, in_=qi), reads=["T1"], writes=["T2"])
        tt("dve", ang, ang, qf, ALU.subtract, ["T1", "T2"], ["T1"])
        ts("dve", qf, ang, 0.5, None, ALU.is_gt, None, ["T1"], ["T2"])
        tt("dve", ang, ang, qf, ALU.subtract, ["T1", "T2"], ["T1"])
        ts("dve", qf, ang, -0.5, None, ALU.is_lt, None, ["T1"], ["T2"])
        tt("dve", ang, ang, qf, ALU.add, ["T1", "T2"], ["T1"])
        for d in range(2):
            eA = 16 if d == 0 else 0
            p.op("dve", lambda e, d=d, eA=eA: e.tensor_copy(out=r16[:, d], in_=ang[:, d, :, eA]),
                 reads=["T1"], writes=["r16"])
        sinv = qf
        act(sinv, ang, AF.Sin, ["T1"], ["T2"], scale=TWO_PI)
        act(ang, ang, AF.Abs, ["T1"], ["T1"])
        cosv = T1[0:64, 784:784 + 784].rearrange("p (a b c) -> p a b c", a=2, b=8)
        act(cosv, ang, AF.Sin, ["T1"], ["T1"], scale=-TWO_PI, bias=1.5707963267948966)
        act(lgm, lgm, AF.Exp, ["T2"], ["T2"])
        for d in range(2):
            eA = 16 if d == 0 else 0
            p.op("dve", lambda e, d=d, eA=eA: e.tensor_copy(out=m16[:, d], in_=lgm[:, d, :, eA]),
                 reads=["T2"], writes=["m16"])
        tt("dve", Lre, lgm, cosv, ALU.mult, ["T2", "T1"], ["Lre"])
        tt("pool", Lim, lgm, sinv, ALU.mult, ["T2", "T2"], ["Lim"])
        nr, ni, den, cre, cim, t_a, t_b = (sm[:, i] for i in range(3, 10))
        for d in range(2):
            e1 = 1 if d == 0 else 15
            ts("dve", nr[:, d], Lre[:, d, :, e1], -1.0, None, ALU.add, None, ["Lre"], ["sm3"])
            p.op("dve", lambda e, d=d, e1=e1: e.tensor_copy(out=ni[:, d], in_=Lim[:, d, :, e1]), reads=["Lim"], writes=["sm4"])
        tt("dve", den, lr, lr, ALU.mult, ["lamP"], ["sm5"])
        tt("dve", t_a, li, li, ALU.mult, ["lamP"], ["sm8"])
        tt("dve", den, den, t_a, ALU.add, ["sm5", "sm8"], ["sm5"])
        p.op("dve", lambda e: e.reciprocal(out=den, in_=den), reads=["sm5"], writes=["sm5"])
        tt("dve", cre, nr, lr, ALU.mult, ["sm3", "lamP"], ["sm6"])
        tt("dve", t_a, ni, li, ALU.mult, ["sm4", "lamP"], ["sm8"])
        tt("dve", cre, cre, t_a, ALU.add, ["sm6", "sm8"], ["sm6"])
        tt("dve", cre, cre, den, ALU.mult, ["sm6", "sm5"], ["sm6"])
        tt("dve", cim, ni, lr, ALU.mult, ["sm4", "lamP"], ["sm7"])
        tt("dve", t_a, nr, li, ALU.mult, ["sm3", "lamP"], ["sm8"])
        tt("dve", cim, cim, t_a, ALU.subtract, ["sm7", "sm8"], ["sm7"])
        tt("dve", cim, cim, den, ALU.mult, ["sm7", "sm5"], ["sm7"])
        for d in range(2):
            x1 = T1v(64, [8, 16])
            x2 = T2v(64, [8, 16])
            crb, cib = bc(cre[:, d], 2, 16), bc(cim[:, d], 2, 16)
            tt("dve", x1, crb, bP[:, 0], ALU.mult, ["sm6", "bP"], ["T1"])
            tt("dve", x2, cib, bP[:, 1], ALU.mult, ["sm7", "bP"], ["T2"])
            tt("dve", bbar[:, d, 0], x1, x2, ALU.subtract, ["T1", "T2"], ["bbar"])
            tt("dve", x1, crb, bP[:, 1], ALU.mult, ["sm6", "bP"], ["T1"])
            tt("dve", x2, cib, bP[:, 0], ALU.mult, ["sm7", "bP"], ["T2"])
            tt("dve", bbar[:, d, 1], x1, x2, ALU.add, ["T1", "T2"], ["bbar"])
        for d in range(2):
            for gh in range(2):
                gs = slice(gh * 4, gh * 4 + 4)
                x1 = T1v(64, [4, 17, 16])
                x2 = T2v(64, [4, 17, 16])
                Lr, Li = bc(Lre[:, d, gs, 0:17], 3, 16), bc(Lim[:, d, gs, 0:17], 3, 16)
                Cr, Ci = bc(cP[:, 0, gs], 2, 17), bc(cP[:, 1, gs], 2, 17)
                tt("dve", x1, Cr, Lr, ALU.mult, ["cP", "Lre"], ["T1"])
                tt("pool", x2, Ci, Li, ALU.mult, ["cP", "Lim"], ["T2"])
                tt("dve", Ytab[:, d, 0, gs], x1, x2, ALU.subtract, ["T1", "T2"], ["Ytab"])
                tt("dve", x1, Cr, Li, ALU.mult, ["cP", "Lim"], ["T1"])
                tt("pool", x2, Ci, Lr, ALU.mult, ["cP", "Lre"], ["T2"])
                ts("dve", x1, x1, -1.0, None, ALU.mult, None, ["T1"], ["T1"])
                tt("dve", Ytab[:, d, 1, gs], x1, x2, ALU.subtract, ["T1", "T2"], ["Ytab"])

        def gen_xw(e0):
            for d in range(2):
                for gh in range(2):
                    gs = slice(gh * 4, gh * 4 + 4)
                    x1 = T1v(64, [4, 16, 16])
                    x2 = T2v(64, [4, 16, 16])
                    Lr, Li = bc(Lre[:, d, gs, e0:e0 + 16], 3, 16), bc(Lim[:, d, gs, e0:e0 + 16], 3, 16)
                    Br, Bi = bc(bbar[:, d, 0, gs], 2, 16), bc(bbar[:, d, 1, gs], 2, 16)
                    tt("dve", x1, Lr, Br, ALU.mult, ["Lre", "bbar"], ["T1"])
                    tt("pool", x2, Li, Bi, ALU.mult, ["Lim", "bbar"], ["T2"])
                    tt("dve", XWt[:, d, 0, gs], x1, x2, ALU.subtract, ["T1", "T2"], ["XWt"])
                    tt("dve", x1, Lr, Bi, ALU.mult, ["Lre", "bbar"], ["T1"])
                    tt("pool", x2, Li, Br, ALU.mult, ["Lim", "bbar"], ["T2"])
                    tt("dve", XWt[:, d, 1, gs], x1, x2, ALU.add, ["T1", "T2"], ["XWt"])

        gen_xw(17)
        for g in range(8):
            for half in range(2):
                pi = next_ps()
                for d in range(2):
                    n0 = 0 if d == 0 else 1
                    for ri in range(2):
                        p.op("pe", lambda e, g=g, half=half, d=d, ri=ri, n0=n0, pi=pi: e.matmul(
                            ps[pi][:, d * 256:(d + 1) * 256],
                            lhsT=XWf[:, d, ri, g, half * 128:(half + 1) * 128],
                            rhs=Yf[:, d, ri, g, n0 * 16:n0 * 16 + 256],
                            start=(ri == 0), stop=(ri == 1)),
                            reads=["XWt", "Ytab"], writes=[("ps", pi)])
                m1 = T1v(128, [512])
                tt("dve", m1, ps[pi][:, :], mask_sb[:, half].rearrange("p a b -> p (a b)"), ALU.mult,
                   [("ps", pi), "s5mask"], ["T1"])
                tt("pool", TgT[:, half, g, :], m1[:, 0:256], m1[:, 256:512], ALU.add, ["T1"], ["TgT"])
        gen_xw(33)
        for d in range(2):
            for hh in range(2):
                ri = hh
                pi = next_ps()
                pb = ps[pi][:, :].bitcast(BF16)
                for half in range(2):
                    for g in range(8):
                        col = (half * 8 + g) * 64
                        p.op("pe", lambda e, d=d, ri=ri, half=half, g=g, col=col, pb=pb: e.transpose(
                            out=pb[:, col:col + 64], in_=XWf[:, d, ri, g, half * 128:(half + 1) * 128],
                            identity=identb[0:64, 0:64]), reads=["XWt", "identb"], writes=[("ps", pi)])
                p.op("act", lambda e, d=d, ri=ri, pb=pb: e.activation(
                    out=WmT[:, :, d, ri, :, :], in_=pb.rearrange("p (h g q) -> p h g q", h=2, g=8), func=AF.Copy),
                    reads=[("ps", pi)], writes=["WmT"])
        for g in range(8):
            pi = next_ps()
            for d in range(2):
                for ri in range(2):
                    c0 = (d * 2 + ri) * 128
                    for half in range(2):
                        p.op("pe", lambda e, g=g, d=d, ri=ri, half=half, c0=c0, pi=pi: e.matmul(
                            ps[pi][0:64, c0:c0 + 128],
                            lhsT=WmT[:, half, d, ri, g, :], rhs=uT[:, g, half, :],
                            start=(half == 0), stop=(half == 1)),
                            reads=["WmT", "uT"], writes=[("ps", pi)])
            p.op("act", lambda e, g=g, pi=pi: e.activation(
                out=Wsb[:, :, :, g, 1:129], in_=ps[pi][0:64, :].rearrange("p (d r c) -> p d r c", d=2, r=2),
                func=AF.Copy), reads=[("ps", pi)], writes=["Wsb"])
        for d in range(2):
            tt("dve", Mdec[:, d], bc(m16[:, d], 2, 130), bc(rmask[:, d], 1, 8), ALU.mult, ["m16", "rmask"], ["Mdec"])
        p.op("pool", lambda e: e.memset(HX[:, :, :, :, 0:1].rearrange("p d r g c -> p (d r g c)"), 0.0),
             reads=[], writes=["XWt"])
        p.op("pool", lambda e: e.memset(HX[:, :, :, :, 129:130].rearrange("p d r g c -> p (d r g c)"), 0.0),
             reads=[], writes=["XWt"])
        for d in range(2):
            a1 = T1[0:64, 0:1024].rearrange("p (g c) -> p g c", g=8)
            a2 = T2[0:64, 0:1024].rearrange("p (g c) -> p g c", g=8)
            qi2 = T2[0:64, 0:1024].bitcast(I32).rearrange("p (g c) -> p g c", g=8)
            tt("dve", a1, bc(r16[:, d], 2, 128), bc(cidx[:, d], 1, 8), ALU.mult, ["r16", "cidx"], ["T1"])
            p.op("dve", lambda e: e.tensor_copy(out=qi2, in_=a1), reads=["T1"], writes=["T2"])
            p.op("dve", lambda e: e.tensor_copy(out=TWC, in_=qi2), reads=["T2"], writes=["TWC"])
            tt("dve", a1, a1, TWC, ALU.subtract, ["T1", "TWC"], ["T1"])
            ts("dve", TWC, a1, 0.5, None, ALU.is_gt, None, ["T1"], ["TWC"])
            tt("dve", a1, a1, TWC, ALU.subtract, ["T1", "TWC"], ["T1"])
            ts("dve", TWC, a1, -0.5, None, ALU.is_lt, None, ["T1"], ["TWC"])
            tt("dve", a1, a1, TWC, ALU.add, ["T1", "TWC"], ["T1"])
            act(TWS, a1, AF.Sin, ["T1"], ["TWS"], scale=TWO_PI)
            act(a1, a1, AF.Abs, ["T1"], ["T1"])
            act(TWC, a1, AF.Sin, ["T1"], ["TWC"], scale=-TWO_PI, bias=1.5707963267948966)
            Wre = Wsb[:, d, 0, :, 1:129]
            Wim = Wsb[:, d, 1, :, 1:129]
            tt("dve", a1, Wre, TWC, ALU.mult, ["Wsb", "TWC"], ["T1"])
            tt("pool", a2, Wim, TWS, ALU.mult, ["Wsb", "TWS"], ["T2"])
            tt("dve", a1, a1, a2, ALU.add, ["T1", "T2"], ["T1"])
            tt("pool", a2, Wre, TWS, ALU.mult, ["Wsb", "TWS"], ["T2"])
            cp_ = lambda eng, o, i_, r, w: p.op(eng, lambda e: e.tensor_copy(out=o, in_=i_), reads=r, writes=w)
            cp_("dve", Wre, a1, ["T1", "T2"], ["Wsb"])
            tt("dve", a1, Wim, TWC, ALU.mult, ["Wsb", "TWC"], ["T1"])
            tt("dve", Wim, a1, a2, ALU.subtract, ["T1", "T2"], ["Wsb"])
            for ri in range(2):
                seg = Wsb[:, d, ri].rearrange("p g c -> p (g c)")
                dec = Mdec[:, d].rearrange("p g c -> p (g c)")
                if d == 1:
                    seg = seg[:, ::-1]
                    dec = dec[:, ::-1]
                p.op("dve", lambda e, seg=seg, dec=dec: e.tensor_tensor_scan(
                    out=seg, data0=dec, data1=seg, initial=0.0, op0=ALU.mult, op1=ALU.add),
                    reads=["Wsb", "Mdec"], writes=["Wsb"])
            tt("dve", a1, Wre, TWC, ALU.mult, ["Wsb", "TWC"], ["T1"])
            tt("pool", a2, Wim, TWS, ALU.mult, ["Wsb", "TWS"], ["T2"])
            tt("dve", HX[:, d, 0, :, 1:129], a1, a2, ALU.subtract, ["T1", "T2"], ["XWt"])
            tt("dve", a1, Wim, TWC, ALU.mult, ["Wsb", "TWC"], ["T1"])
            tt("pool", a2, Wre, TWS, ALU.mult, ["Wsb", "TWS"], ["T2"])
            tt("dve", HX[:, d, 1, :, 1:129], a1, a2, ALU.add, ["T1", "T2"], ["XWt"])
        ytm = T2[:, 0:1024].bitcast(BF16).rearrange("p (t c) -> p t c", t=16)
        WB = MS[:, wmt_off:wmt_off + 4096].bitcast(F32)

        def gelu_chain(g2, R, rk):
            pi = next_ps()
            for gg in range(2):
                g = g2 * 2 + gg
                o = ps[pi][:, gg * 256:(gg + 1) * 256]
                mms = [(uT[:, g, 0, :], TgT[:, 0, g, :]), (uT[:, g, 1, :], TgT[:, 1, g, :])]
                for ri in range(2):
                    mms.append((HX[:, 0, ri, g, 0:128], Yf[:, 0, ri, g, 16:272]))
                    mms.append((HX[:, 1, ri, g, 2:130], Yf[:, 1, ri, g, 0:256]))
                for i, (l_, r_) in enumerate(mms):
                    p.op("pe", lambda e, o=o, l_=l_, r_=r_, i=i, n=len(mms): e.matmul(
                        o, lhsT=l_, rhs=r_, start=(i == 0), stop=(i == n - 1)),
                        reads=["uT", "TgT", "XWt", "Ytab"], writes=[("ps", pi)])
            yield
            yv = R[:, 0:512].rearrange("p (g t h) -> p g t h", g=2, t=16)
            z1 = R[:, 512:1024].rearrange("p (g t h) -> p g t h", g=2, t=16)
            z2 = R[:, 1024:1536].rearrange("p (g t h) -> p g t h", g=2, t=16)
            for gg in range(2):
                g = g2 * 2 + gg
                uview = u_tm[:, g, :, :]
                tt("dve", z1[:, gg], uview, bc(dbc[:, g * 16:(g + 1) * 16], 1, 16), ALU.mult,
                   ["u_tm", "dbc"], [rk])
                tt("dve", yv[:, gg], z1[:, gg],
                   ps[pi][:, gg * 256:(gg + 1) * 256].rearrange("p (t h) -> p t h", t=16), ALU.add,
                   [rk, ("ps", pi)], [rk])
            yield
            yf = R[:, 0:512]
            z1f = R[:, 512:1024]
            z2f = R[:, 1024:1536]
            tt("pool", z1f, yf, yf, ALU.mult, [rk], [rk])
            yield
            ts("dve", z1f, z1f, 0.044715, 1.0, ALU.mult, ALU.add, [rk], [rk])
            yield
            tt("pool", z1f, z1f, yf, ALU.mult, [rk], [rk])
            yield
            act(z2f, z1f, AF.Sigmoid, [rk], [rk], scale=1.5957691216057308)
            yield
            for gg in range(2):
                g = g2 * 2 + gg
                tt("dve", ytm[:, :, g * 16:(g + 1) * 16], yv[:, gg], z2[:, gg], ALU.mult, [rk], ["T2"])
            yield

        run_gens([gelu_chain(0, T1, "T1"), gelu_chain(1, WB, "WmT")])
        run_gens([gelu_chain(2, T1, "T1"), gelu_chain(3, WB, "WmT")])
        ygT = T1[:, 0:1024].bitcast(BF16)
        for h2 in range(2):
            pi = next_ps()
            pb = ps[pi][:, :].bitcast(BF16)
            for t8 in range(8):
                t_ = h2 * 8 + t8
                p.op("pe", lambda e, t_=t_, t8=t8, pb=pb: e.transpose(
                    out=pb[:, t8 * 128:(t8 + 1) * 128], in_=ytm[:, t_, :], identity=identb[:]),
                    reads=["T2", "identb"], writes=[("ps", pi)])
            p.op("act", lambda e, h2=h2, pb=pb: e.activation(out=ygT[:, h2 * 1024:(h2 + 1) * 1024], in_=pb, func=AF.Copy),
                 reads=[("ps", pi), "T1", "T1", "T1"], writes=["T1"])
        p.dma("sp", ygT_d[gb], ygT, "ygw", reads=["T1"], writes=[("ygd", gb)])

    p.barrier()
    ar.reset()
    ygA = ar.alloc(128, [8, S], BF16)
    wg = [ar.alloc(128, [8, 128], BF16) for _ in range(2)]
    sgb = [ar.alloc(128, [512], F32) for _ in range(2)]
    mxb = [ar.alloc(128, [512], BF16) for _ in range(2)]
    p.dma("sp", ygA, ygT_d.rearrange("k p t -> p k t"), "ygld", writes=["ygA"])
    for nt in range(8):
        b = nt % 2
        p.dma("pool", wg[b], wglu_d[:, :, nt * 128:(nt + 1) * 128], "wg%d" % b, writes=[("wg", b)])
        for tb in range(4):
            pi = next_ps()
            for kt in range(8):
                p.op("pe", lambda e, kt=kt, pi=pi, tb=tb, b=b: e.matmul(
                    ps[pi][:, :], lhsT=wg[b][:, kt, :], rhs=ygA[:, kt, tb * 512:(tb + 1) * 512],
                    start=(kt == 0), stop=(kt == 7)), reads=[("wg", b), "ygA"], writes=[("ps", pi)])
            r = tb % 2
            act(sgb[r], ps[pi][:, :], AF.Sigmoid, [("ps", pi), "bglu"], [("sgb", r)], bias=bglu[:, nt:nt + 1])
            tt("dve", mxb[r], sgb[r], ygA[:, nt, tb * 512:(tb + 1) * 512], ALU.mult, [("sgb", r), "ygA"], [("mxb", r)])
            p.dma("sp", mixT_d[:, nt, tb * 512:(tb + 1) * 512], mxb[r], "mxw%d" % r, reads=[("mxb", r)],
                  writes=["mixd"])


def prep_inputs(inputs):
    f = lambda a: np.ascontiguousarray(np.asarray(a, dtype=np.float32))
    sh = {}
    ada_w = f(inputs["ada_w"])[0]
    sh["ada_w"] = f(ada_w.reshape(KT, 128, 12, 1024).transpose(1, 2, 0, 3))
    ada_b = f(inputs["ada_b"])[0]
    sh["ada_b_fm"] = f(ada_b.reshape(96, 128).T)
    gb = np.concatenate([ada_b[2 * D:3 * D], ada_b[5 * D:6 * D]])
    sh["ada_b_bc"] = f(np.broadcast_to(gb[None, :], (128, 2 * D)))
    sh["g1_fm"] = f(f(inputs["norm1_gain"])[0].reshape(KT, 128).T)
    sh["g2_fm"] = f(f(inputs["norm2_gain"])[0].reshape(KT, 128).T)
    sh["ident"] = np.eye(128, dtype=np.float32)
    sh["w_out"] = f(f(inputs["w_out"])[0].reshape(KT, 128, 4, 512).transpose(1, 2, 0, 3))
    sh["ffn_w1"] = f(f(inputs["ffn_w1"])[0].reshape(KT, 128, FFN // 256, 256).transpose(1, 2, 0, 3))
    sh["ffn_w2"] = f(f(inputs["ffn_w2"])[0].reshape(8, 8, 128, 4, 512).transpose(2, 3, 0, 1, 4))
    sh["fgain_bc"] = f(np.broadcast_to(f(inputs["final_gain"])[None, :], (128, D)))
    w_in_pad = np.zeros((D, 36 * 128), np.float32)
    w_in_pad[:, :PROJ] = f(inputs["w_in"])[0]
    sh["w_in_h"] = f(w_in_pad.reshape(KT, 128, 36, 128).transpose(1, 2, 0, 3))
    lre, lim = f(inputs["s5_lambda_re"])[0], f(inputs["s5_lambda_im"])[0]
    lst = f(inputs["s5_log_step"])[0]
    lam = np.stack([lre, lim, np.broadcast_to(lst[:, :, None], lre.shape)], 0)
    sh["s5_lamP"] = f(lam.reshape(3, 2, 8, 8, 64).transpose(2, 4, 0, 1, 3))
    bre, bim = f(inputs["s5_b_re"])[0], f(inputs["s5_b_im"])[0]
    bb = np.stack([bre, bim], 0)
    sh["s5_bP"] = f(bb.reshape(2, 8, 8, 64, 16).transpose(1, 3, 0, 2, 4))
    cre_, cim_ = f(inputs["s5_c_re"])[0], f(inputs["s5_c_im"])[0]
    cc = np.stack([cre_, cim_], 0)
    sh["s5_cP"] = f(cc.reshape(2, 8, 8, 16, 64).transpose(1, 4, 0, 2, 3))
    n17 = np.arange(17, dtype=np.float32)
    m16 = np.arange(16, dtype=np.float32)
    E = np.stack([np.concatenate([n17, -m16, 15 - m16]), np.concatenate([16 - n17, m16 - 15, m16])], 0)
    sh["s5_E"] = f(np.broadcast_to(E[None], (64, 2, 49)))
    sh["s5_E2pi"] = f(np.broadcast_to((E / (2 * np.pi))[None], (64, 2, 49)))
    s8 = np.arange(128) // 16
    tt_ = np.arange(256) // 16
    mask = np.zeros((128, 2, 2, 256), np.float32)
    for half in range(2):
        sv = half * 8 + s8
        mask[:, half, 0, :] = (tt_[None, :] >= sv[:, None])
        mask[:, half, 1, :] = (tt_[None, :] <= sv[:, None])
    sh["s5_mask"] = mask
    ci = np.arange(128, dtype=np.float32)
    sh["s5_cidx"] = f(np.broadcast_to(np.stack([ci, 127 - ci], 0)[None], (64, 2, 128)))
    rm = np.ones((2, 130), np.float32)
    rm[0, 0] = 0.0
    rm[1, 129] = 0.0
    sh["s5_rmask"] = f(np.broadcast_to(rm[None], (64, 2, 130)))
    sh["s5_d_bc"] = f(np.broadcast_to(f(inputs["s5_d"])[0][None, :], (128, 1024)))
    mup, mun = f(inputs["rk_shift_prev"])[0], f(inputs["rk_shift_next"])[0]
    prm = np.zeros((128, 112), np.float32)
    for t_ in range(28):
        n_ = min(128, 3488 - t_ * 128)
        prm[:n_, t_] = mup[t_ * 128:t_ * 128 + n_]
        prm[:n_, 28 + t_] = mun[t_ * 128:t_ * 128 + n_]
    w0_, a0_ = f(inputs["rk_w0"])[0], f(inputs["rk_a0"])[0]
    for d_ in range(2):
        prm[:, 56 + d_ * 8:56 + d_ * 8 + 8] = w0_[d_].reshape(8, 128).T
        prm[:, 72 + d_ * 8:72 + d_ * 8 + 8] = a0_[d_].reshape(8, 128).T
    prm[:, 88:96] = f(inputs["rk_k_k"])[0].reshape(8, 128).T
    prm[:, 96:104] = f(inputs["rk_k_a"])[0].reshape(8, 128).T
    prm[:, 104:112] = f(inputs["rk_r_k"])[0].reshape(8, 128).T
    sh["rk_prm"] = prm
    sh["rk_wup"] = f(f(inputs["rk_w_up"])[0].reshape(128, 1024))
    sh["rk_aup"] = f(f(inputs["rk_a_up"])[0].reshape(128, 1024))
    gup = f(inputs["rk_g_up"])[0]
    sh["rk_gup1"] = f(gup[0:128])
    sh["rk_gup2"] = f(gup[128:160])
    sh["rk_lng"] = f(np.repeat(f(inputs["rk_ln_gain"])[0].reshape(8, 2, 64), 64, axis=1))
    sh["rk_lnb"] = f(np.repeat(f(inputs["rk_ln_bias"])[0].reshape(8, 2, 64), 64, axis=1))
    sm_ = np.ones((128, 512), np.float32)
    sm_[:, ::64] = 0.0
    sh["rk_smask"] = sm_
    ii = np.arange(64)
    m5 = np.zeros((128, 2, 320), np.float32)
    for d_ in range(2):
        if d_ == 0:
            strict = (ii[:, None] < ii[None, :]).astype(np.float32)
            incl = (ii[:, None] <= ii[None, :]).astype(np.float32)
        else:
            strict = (ii[:, None] > ii[None, :]).astype(np.float32)
            incl = (ii[:, None] >= ii[None, :]).astype(np.float32)
        row = np.concatenate([-strict, incl, strict, incl, -strict.T], axis=1)
        m5[0:64, d_] = row
        m5[64:128, d_] = row
    sh["rk_mask5"] = m5
    sh["rk_i64"] = f(np.concatenate([np.eye(64), np.eye(64)], 0))
    bo = np.zeros((128, 128), np.float32)
    bo[0:64, 0:64] = 1.0
    bo[64:128, 64:128] = 1.0
    sh["rk_bones"] = bo
    sh["w_glu_h"] = f(f(inputs["s5_w_glu"])[0].reshape(8, 128, 1024).transpose(1, 0, 2))
    sh["b_glu_fm"] = f(f(inputs["s5_b_glu"])[0].reshape(8, 128).T)
    per = []
    x = f(inputs["x"])
    c = f(inputs["c"])
    for b in range(8):
        per.append({"x": x[b], "c_col": f(c[b].reshape(KT, 128).T)})
    return sh, per


_NC_CACHE = {}


def kernel(**inputs):
    sh, per = prep_inputs(inputs)
    if "main" not in _NC_CACHE:
        _NC_CACHE["main"] = build_program()
    nc = _NC_CACHE["main"]
    in_maps = [dict(sh, **per[b]) for b in range(8)]
    res = run_bass_kernel_spmd(nc, in_maps, core_ids=list(range(8)))
    out = np.stack([np.asarray(r["out"], dtype=np.float32) for r in res.results], axis=0)
    return out
```
